# Optimizing a Trainium2 kernel written in Bass

```python
import math
import jax, jax.numpy as jnp
from jax import lax
import numpy as np

D_MODEL = 4096
BATCH = 1
SEQ = 8192
DEPTH = 4

GRID_W = 64
CTX_LEN = 256
HEAD_DIM = 128
MIX_WIDTH = D_MODEL
EPS = 1e-6
ROPE_THETA = 10000.0
HY_WIDTH = MIX_WIDTH // 2
HY_ORDER = 2
HY_SHORT = 3
HY_EMB_BANDS = 16
HY_EMB_DIM = 1 + 2 * HY_EMB_BANDS
HY_FILTER_HIDDEN = 64
HY_FAST_DECAY = 0.3
HY_SLOW_DECAY = 1.5
HY_DECAY_TARGET = 1e-2
HY_FILTER_OUT_SCALE = 0.005
GQA_WIDTH = MIX_WIDTH - HY_WIDTH
GQA_Q_HEADS = GQA_WIDTH // HEAD_DIM
GQA_KV_HEADS = GQA_Q_HEADS // 4
GQA_KV_WIDTH = GQA_KV_HEADS * HEAD_DIM
Q_BLOCK = 128
OFF_HY_GATE = (HY_ORDER + 1) * HY_WIDTH
OFF_Q = OFF_HY_GATE + HY_WIDTH
OFF_K = OFF_Q + GQA_WIDTH
OFF_V = OFF_K + GQA_KV_WIDTH
OFF_AT_GATE = OFF_V + GQA_KV_WIDTH
EVEN_IN = OFF_AT_GATE + GQA_WIDTH
NA_WIDTH = MIX_WIDTH
NA_HEADS = NA_WIDTH // HEAD_DIM
NA_KH_MAX = 8
NA_KW = 16
ODD_IN = 4 * NA_WIDTH
N_EVEN = (DEPTH + 1) // 2
N_ODD = DEPTH // 2

kernel_name = "hyena_gqa_natten_hybrid_dit"


def rms_norm(x, g):
    xf = x.astype(jnp.float32)
    y = xf * lax.rsqrt(jnp.mean(xf * xf, axis=-1, keepdims=True) + EPS)
    return (y * g.astype(jnp.float32)).astype(x.dtype)


def ada_mod(cvec, w, b, n):
    d = w.shape[0]
    m = jax.nn.silu(cvec) @ w[:, : n * d] + b[: n * d]
    return jnp.split(m, n, axis=-1)


def modulate(h, shift, scale):
    return h * (1.0 + scale[:, None, :]) + shift[:, None, :]


def short_conv(u, w):
    L = u.shape[1]
    pad = HY_SHORT // 2
    up = jnp.pad(u, ((0, 0), (pad, HY_SHORT - 1 - pad), (0, 0)))
    out = up[:, 0:L] * w[0]
    for j in range(1, HY_SHORT):
        out = out + up[:, j:j + L] * w[j]
    return out


def hyena_filters(L, w1, b1, w2, b2, w3, freq, decay):
    f32 = jnp.float32
    t = jnp.linspace(0.0, 1.0, L, dtype=f32)[:, None]
    w = (2.0 * math.pi / L) * jnp.arange(L, dtype=f32)[:, None]
    bands = jnp.linspace(1e-4, HY_EMB_BANDS - 1, HY_EMB_BANDS, dtype=f32)[None, :]
    z = jnp.concatenate([t, jnp.cos(bands * w), -jnp.sin(bands * w)], axis=-1)
    fr = freq.astype(f32)
    hdn = jnp.sin(fr * (z @ w1.astype(f32) + b1.astype(f32)))
    hdn = jnp.sin(fr * (hdn @ w2.astype(f32) + b2.astype(f32)))
    filt = (hdn @ w3.astype(f32)).reshape(L, HY_ORDER, 2, -1)
    window = jnp.exp(-t[:, :, None] * jnp.abs(decay.astype(f32))[None])
    filt = filt * window[:, :, None, :]
    fwd, bwd = filt[:, :, 0], filt[:, :, 1]
    two_sided = jnp.concatenate([fwd, jnp.zeros_like(fwd[:1]), bwd[:0:-1]], axis=0)
    return jnp.fft.rfft(two_sided, axis=0)


def long_conv(z, hf, d):
    L = z.shape[1]
    zf = jnp.fft.rfft(z, n=2 * L, axis=1)
    return jnp.fft.irfft(zf * hf[None], n=2 * L, axis=1)[:, :L] + z * d


def hyena(u, conv_w, filt_f, dskip):
    u = short_conv(u, conv_w).astype(jnp.float32)
    pieces = jnp.split(u, HY_ORDER + 1, axis=-1)
    z = pieces[0]
    ds = dskip.astype(jnp.float32)
    for o in range(HY_ORDER):
        z = pieces[o + 1] * long_conv(z, filt_f[:, o], ds[o])
    return z


def axial_rope(x, rows, cols):
    half = HEAD_DIM // 2
    quarter = half // 2
    inv = ROPE_THETA ** (-jnp.arange(quarter, dtype=jnp.float32) / quarter)

    def rot(xa, pos):
        ang = pos.astype(jnp.float32)[:, None] * inv[None]
        cos = jnp.cos(ang)[None, :, None, :]
        sin = jnp.sin(ang)[None, :, None, :]
        x1, x2 = xa[..., :quarter], xa[..., quarter:]
        return jnp.concatenate([x1 * cos - x2 * sin, x1 * sin + x2 * cos], axis=-1)

    xf = x.astype(jnp.float32)
    out = jnp.concatenate([rot(xf[..., :half], rows), rot(xf[..., half:], cols)], axis=-1)
    return out.astype(x.dtype)


def dense_attention(q, k, v):
    B, Lq, HQ, hd = q.shape
    HKV = k.shape[2]
    qg = q.reshape(B, Lq, HKV, HQ // HKV, hd)
    s = jnp.einsum('bqkgd,bskd->bkgqs', qg, k, preferred_element_type=jnp.float32) * (hd ** -0.5)
    p = jax.nn.softmax(s, axis=-1).astype(v.dtype)
    return jnp.einsum('bkgqs,bskd->bqkgd', p, v).reshape(B, Lq, HQ * hd)


def block_attention(q, k, v):
    B, L, HQ, hd = q.shape
    nb = L // Q_BLOCK
    qb = q.reshape(B, nb, Q_BLOCK, HQ, hd).transpose(1, 0, 2, 3, 4)
    o = lax.map(lambda qq: dense_attention(qq, k, v), qb)
    return o.transpose(1, 0, 2, 3).reshape(B, L, HQ * hd)


def neighbourhood_attention(q, k, v, kc, vc, rel_bias):
    B, L, H, hd = q.shape
    rows = L // GRID_W
    kh = min(NA_KH_MAX, rows)
    kw = NA_KW
    scale = hd ** -0.5
    qg = q.reshape(B, rows, GRID_W, H, hd)
    kg = k.reshape(B, rows, GRID_W, H, hd)
    vg = v.reshape(B, rows, GRID_W, H, hd)
    cols = jnp.arange(GRID_W)
    col_start = jnp.clip(cols - kw // 2, 0, GRID_W - kw)
    col_idx = col_start[:, None] + jnp.arange(kw)[None]
    dcol = col_idx - cols[:, None] + (kw - 1)
    bias_cols = rel_bias[:, :, dcol]

    def one_row(r):
        rs = jnp.clip(r - kh // 2, 0, rows - kh)
        kb = lax.dynamic_slice_in_dim(kg, rs, kh, axis=1)
        vb = lax.dynamic_slice_in_dim(vg, rs, kh, axis=1)
        kwin = kb[:, :, col_idx]
        vwin = vb[:, :, col_idx]
        qr = lax.dynamic_index_in_dim(qg, r, axis=1, keepdims=False)
        drow = rs + jnp.arange(kh) - r + (NA_KH_MAX - 1)
        bias = jnp.take(bias_cols, drow, axis=1).transpose(0, 2, 1, 3)
        s_win = jnp.einsum('bchd,bicjhd->bhcij', qr, kwin, preferred_element_type=jnp.float32) * scale
        s_win = s_win + bias[None].astype(jnp.float32)
        s_ctx = jnp.einsum('bchd,bshd->bhcs', qr, kc, preferred_element_type=jnp.float32) * scale
        s = jnp.concatenate([s_win.reshape(B, H, GRID_W, kh * kw), s_ctx], axis=-1)
        p = jax.nn.softmax(s, axis=-1).astype(v.dtype)
        p_win = p[..., : kh * kw].reshape(B, H, GRID_W, kh, kw)
        p_ctx = p[..., kh * kw:]
        return (jnp.einsum('bhcij,bicjhd->bchd', p_win, vwin)
                + jnp.einsum('bhcs,bshd->bchd', p_ctx, vc))

    o = lax.map(one_row, jnp.arange(rows))
    return o.transpose(1, 0, 2, 3, 4).reshape(B, L, H * hd)


def even_mixer(h, hc, need_ctx_out, rows_pos, cols_pos, w_in, w_out, hy_conv, hy_w1, hy_b1,
               hy_w2, hy_b2, hy_w3, hy_freq, hy_decay, hy_dskip, q_norm, k_norm):
    B, L, _ = h.shape
    Lc = hc.shape[1]
    u = h @ w_in
    q = rms_norm(u[..., OFF_Q:OFF_K].reshape(B, L, GQA_Q_HEADS, HEAD_DIM), q_norm)
    k = rms_norm(u[..., OFF_K:OFF_V].reshape(B, L, GQA_KV_HEADS, HEAD_DIM), k_norm)
    v = u[..., OFF_V:OFF_AT_GATE].reshape(B, L, GQA_KV_HEADS, HEAD_DIM)
    q = axial_rope(q, rows_pos, cols_pos)
    k = axial_rope(k, rows_pos, cols_pos)
    if need_ctx_out:
        uc = hc @ w_in
        kvc = uc[..., OFF_K:OFF_AT_GATE]
    else:
        kvc = hc @ w_in[:, OFF_K:OFF_AT_GATE]
    kc = rms_norm(kvc[..., :GQA_KV_WIDTH].reshape(B, Lc, GQA_KV_HEADS, HEAD_DIM), k_norm)
    vc = kvc[..., GQA_KV_WIDTH:].reshape(B, Lc, GQA_KV_HEADS, HEAD_DIM)
    att = block_attention(q, jnp.concatenate([kc, k], axis=1), jnp.concatenate([vc, v], axis=1))
    filt = (hy_w1, hy_b1, hy_w2, hy_b2, hy_w3, hy_freq, hy_decay)
    hy = hyena(u[..., :OFF_HY_GATE], hy_conv, hyena_filters(L, *filt), hy_dskip).astype(h.dtype)
    y = jnp.concatenate([hy * jax.nn.silu(u[..., OFF_HY_GATE:OFF_Q]),
                         att * jax.nn.silu(u[..., OFF_AT_GATE:])], axis=-1) @ w_out
    yc = None
    if need_ctx_out:
        qc = rms_norm(uc[..., OFF_Q:OFF_K].reshape(B, Lc, GQA_Q_HEADS, HEAD_DIM), q_norm)
        att_c = dense_attention(qc, kc, vc)
        hy_c = hyena(uc[..., :OFF_HY_GATE], hy_conv, hyena_filters(Lc, *filt), hy_dskip).astype(hc.dtype)
        yc = jnp.concatenate([hy_c * jax.nn.silu(uc[..., OFF_HY_GATE:OFF_Q]),
                              att_c * jax.nn.silu(uc[..., OFF_AT_GATE:])], axis=-1) @ w_out
    return y, yc


def odd_mixer(h, hc, need_ctx_out, w_in, w_out, q_norm, k_norm, rel_bias):
    B, L, _ = h.shape
    Lc = hc.shape[1]
    W = NA_WIDTH
    u = h @ w_in
    q = rms_norm(u[..., :W].reshape(B, L, NA_HEADS, HEAD_DIM), q_norm)
    k = rms_norm(u[..., W:2 * W].reshape(B, L, NA_HEADS, HEAD_DIM), k_norm)
    v = u[..., 2 * W:3 * W].reshape(B, L, NA_HEADS, HEAD_DIM)
    if need_ctx_out:
        uc = hc @ w_in
        kvc = uc[..., W:3 * W]
    else:
        kvc = hc @ w_in[:, W:3 * W]
    kc = rms_norm(kvc[..., :W].reshape(B, Lc, NA_HEADS, HEAD_DIM), k_norm)
    vc = kvc[..., W:].reshape(B, Lc, NA_HEADS, HEAD_DIM)
    att = neighbourhood_attention(q, k, v, kc, vc, rel_bias)
    y = (att * jax.nn.silu(u[..., 3 * W:])) @ w_out
    yc = None
    if need_ctx_out:
        qc = rms_norm(uc[..., :W].reshape(B, Lc, NA_HEADS, HEAD_DIM), q_norm)
        att_c = dense_attention(qc, kc, vc)
        yc = (att_c * jax.nn.silu(uc[..., 3 * W:])) @ w_out
    return y, yc


def setup_inputs(seed: int = 0) -> dict:
    key = jax.random.key(seed)
    ks = jax.random.split(key, 26)
    f32 = jnp.float32

    def nrm(k, shape, s):
        return jax.random.normal(k, shape, f32) * s

    decay_lo = abs(math.log(HY_DECAY_TARGET)) / HY_SLOW_DECAY
    decay_hi = abs(math.log(HY_DECAY_TARGET)) / HY_FAST_DECAY
    base_decay = jnp.linspace(decay_lo, decay_hi, HY_WIDTH, dtype=f32)
    return {
        "x": nrm(ks[0], (BATCH, SEQ, D_MODEL), 1.0),
        "c": nrm(ks[1], (BATCH, D_MODEL), 1.0),
        "ctx": nrm(ks[2], (BATCH, CTX_LEN, D_MODEL), 1.0),
        "c_ctx": nrm(ks[3], (D_MODEL,), 1.0),
        "norm_g": 1.0 + nrm(ks[4], (DEPTH, D_MODEL), 0.02),
        "ada_w": nrm(ks[5], (DEPTH, D_MODEL, 3 * D_MODEL), 0.5 * D_MODEL ** -0.5),
        "ada_b": nrm(ks[6], (DEPTH, 3 * D_MODEL), 0.02),
        "e_w_in": nrm(ks[7], (N_EVEN, D_MODEL, EVEN_IN), D_MODEL ** -0.5),
        "e_w_out": nrm(ks[8], (N_EVEN, MIX_WIDTH, D_MODEL), MIX_WIDTH ** -0.5),
        "hy_conv": nrm(ks[9], (N_EVEN, HY_SHORT, (HY_ORDER + 1) * HY_WIDTH), HY_SHORT ** -0.5),
        "hy_w1": nrm(ks[10], (N_EVEN, HY_EMB_DIM, HY_FILTER_HIDDEN), HY_EMB_DIM ** -0.5),
        "hy_b1": nrm(ks[11], (N_EVEN, HY_FILTER_HIDDEN), 0.1),
        "hy_w2": nrm(ks[12], (N_EVEN, HY_FILTER_HIDDEN, HY_FILTER_HIDDEN), HY_FILTER_HIDDEN ** -0.5),
        "hy_b2": nrm(ks[13], (N_EVEN, HY_FILTER_HIDDEN), 0.1),
        "hy_w3": nrm(ks[14], (N_EVEN, HY_FILTER_HIDDEN, HY_ORDER * 2 * HY_WIDTH), HY_FILTER_OUT_SCALE),
        "hy_freq": 1.0 + nrm(ks[15], (N_EVEN, HY_FILTER_HIDDEN), 0.1),
        "hy_decay": base_decay * (1.0 + nrm(ks[16], (N_EVEN, HY_ORDER, HY_WIDTH), 0.05)),
        "hy_dskip": nrm(ks[17], (N_EVEN, HY_ORDER, HY_WIDTH), 0.5),
        "gqa_q_norm": 1.0 + nrm(ks[18], (N_EVEN, HEAD_DIM), 0.02),
        "gqa_k_norm": 1.0 + nrm(ks[19], (N_EVEN, HEAD_DIM), 0.02),
        "o_w_in": nrm(ks[20], (N_ODD, D_MODEL, ODD_IN), D_MODEL ** -0.5),
        "o_w_out": nrm(ks[21], (N_ODD, NA_WIDTH, D_MODEL), NA_WIDTH ** -0.5),
        "na_q_norm": 1.0 + nrm(ks[22], (N_ODD, HEAD_DIM), 0.02),
        "na_k_norm": 1.0 + nrm(ks[23], (N_ODD, HEAD_DIM), 0.02),
        "na_rel_bias": nrm(ks[24], (N_ODD, NA_HEADS, 2 * NA_KH_MAX - 1, 2 * NA_KW - 1), 0.1),
    }


def reference(x, c, ctx, c_ctx, norm_g, ada_w, ada_b, e_w_in, e_w_out, hy_conv, hy_w1, hy_b1,
              hy_w2, hy_b2, hy_w3, hy_freq, hy_decay, hy_dskip, gqa_q_norm, gqa_k_norm,
              o_w_in, o_w_out, na_q_norm, na_k_norm, na_rel_bias):
    L = x.shape[1]
    t = jnp.arange(L)
    rows_pos = t // GRID_W
    cols_pos = t % GRID_W
    xc = ctx
    for layer in range(DEPTH):
        need_ctx_out = layer < DEPTH - 1
        shift, scale, gate = ada_mod(c, ada_w[layer], ada_b[layer], 3)
        mods_c = ada_mod(c_ctx[None], ada_w[layer], ada_b[layer], 3 if need_ctx_out else 2)
        h = modulate(rms_norm(x, norm_g[layer]), shift, scale)
        hc = modulate(rms_norm(xc, norm_g[layer]), mods_c[0], mods_c[1])
        if layer % 2 == 0:
            i = layer // 2
            y, yc = even_mixer(h, hc, need_ctx_out, rows_pos, cols_pos, e_w_in[i], e_w_out[i],
                               hy_conv[i], hy_w1[i], hy_b1[i], hy_w2[i], hy_b2[i], hy_w3[i],
                               hy_freq[i], hy_decay[i], hy_dskip[i], gqa_q_norm[i], gqa_k_norm[i])
        else:
            i = layer // 2
            y, yc = odd_mixer(h, hc, need_ctx_out, o_w_in[i], o_w_out[i], na_q_norm[i],
                              na_k_norm[i], na_rel_bias[i])
        x = x + gate[:, None, :] * y
        if need_ctx_out:
            xc = xc + mods_c[2][:, None, :] * yc
    return x
```

```python
import contextlib
import numpy as np
import concourse.bass as bass
import concourse.mybir as mybir
from concourse.bass_utils import run_bass_kernel_spmd

F32 = mybir.dt.float32
BF16 = mybir.dt.bfloat16
AF = mybir.ActivationFunctionType
ALU = mybir.AluOpType
AX = mybir.AxisListType

NCORES = 8


class Buf:
    __slots__ = ("name", "w", "r")

    def __init__(self, name=""):
        self.name = name
        self.w = None
        self.r = {}


class Sched:
    ENGS = ("pe", "act", "dve", "pool", "sp")
    CE = ("pe", "act", "dve", "pool")

    def __init__(self, nc, dma_slots=None):
        self.nc = nc
        self.q = {e: [] for e in self.ENGS}
        self.cnt = {e: 0 for e in self.CE}
        self.seen = {e: {} for e in self.ENGS}
        self.dq = {"sp": [0, 24], "pool": [0, 8], "act": [0, 8]}
        if dma_slots:
            for k, v in dma_slots.items():
                self.dq[k][1] = v
        self.semkeys = set()
        self.sb_off = 16512
        self.sb_max = 0
        self.nalloc = 0

    def sb(self, shape, dtype, name=None):
        per = int(np.prod(shape[1:])) * (2 if dtype == BF16 else 4)
        off = (self.sb_off + 63) // 64 * 64
        self.nalloc += 1
        t = self.nc.alloc_sbuf_tensor_at(f"{name or 't'}_{self.nalloc}", list(shape), dtype, offset=off)
        self.sb_off = off + per
        self.sb_max = max(self.sb_max, self.sb_off)
        assert self.sb_off <= 229000, f"SBUF overflow {self.sb_off}"
        return t

    def mark(self):
        return self.sb_off

    def release(self, m):
        self.sb_off = m

    def _deps(self, reads, writes):
        deps = {}

        def add(tok):
            if tok is None:
                return
            k, v = tok
            if deps.get(k, 0) < v:
                deps[k] = v

        for b in reads:
            add(b.w)
        for b in writes:
            add(b.w)
            for k, v in b.r.items():
                add((k, v))
        return deps

    def _emit_waits(self, eng, deps, skip_own=False):
        own = "c:" + eng
        for k, v in deps.items():
            if k == own and (eng == "pe" or skip_own):
                continue
            if self.seen[eng].get(k, 0) >= v:
                continue
            self.seen[eng][k] = v
            self.q[eng].append(("w", k, v))

    def _commit(self, tok, reads, writes):
        k, v = tok
        for b in reads:
            if b.r.get(k, 0) < v:
                b.r[k] = v
        for b in writes:
            b.w = tok
            b.r = {}

    def op(self, eng, fn, reads=(), writes=(), skip_own=False):
        deps = self._deps(reads, writes)
        self._emit_waits(eng, deps, skip_own)
        self.cnt[eng] += 1
        own = "c:" + eng
        self.semkeys.add(own)
        tok = (own, self.cnt[eng])
        self.q[eng].append(("o", fn, own, 1))
        self._commit(tok, reads, writes)
        return tok

    def dma(self, out, in_, reads=(), writes=(), q="sp", **kw):
        st = self.dq[q]
        i = st[0]
        st[0] += 1
        R = st[1]
        slot = i % R
        key = f"d:{q}:{slot}"
        self.semkeys.add(key)
        deps = self._deps(reads, writes)
        if i >= R:
            v0 = 16 * (i // R)
            if deps.get(key, 0) < v0:
                deps[key] = v0
        self._emit_waits(q, deps)
        tok = (key, 16 * (i // R + 1))
        self.q[q].append(("o", (lambda e, o=out, s=in_, kw=kw: e.dma_start(out=o, in_=s, **kw)), key, 16))
        self._commit(tok, reads, writes)
        return tok

    def custom(self, eng, fn, key, amount, value, reads=(), writes=()):
        deps = self._deps(reads, writes)
        self._emit_waits(eng, deps)
        self.semkeys.add(key)
        self.q[eng].append(("o", fn, key, amount))
        tok = (key, value)
        self._commit(tok, reads, writes)
        return tok

    def barrier(self):
        allv = {}
        for e in self.CE:
            if self.cnt[e]:
                allv["c:" + e] = self.cnt[e]
        for q, (n, R) in self.dq.items():
            for slot in range(min(n, R)):
                uses = (n - 1 - slot) // R + 1
                allv[f"d:{q}:{slot}"] = 16 * uses
        for k, v in getattr(self, "extra_sems", {}).items():
            allv[k] = v
        for e in self.ENGS:
            self._emit_waits(e, dict(allv), skip_own=True)

    def emit(self):
        nc = self.nc
        with contextlib.ExitStack() as es:
            sems = {}
            for k in sorted(self.semkeys):
                sems[k] = es.enter_context(nc.semaphore(k.replace(":", "_")))
            block = es.enter_context(nc.Block())

            def run(engname):
                def body(e):
                    for it in self.q[engname]:
                        if it[0] == "w":
                            e.wait_ge(sems[it[1]], it[2])
                        else:
                            ins = it[1](e)
                            ins.then_inc(sems[it[2]], it[3])
                return body

            block.tensor(run("pe"))
            block.scalar(run("act"))
            block.vector(run("dve"))
            block.gpsimd(run("pool"))
            block.sync(run("sp"))


L = 8192
LC = 256
NT = L + LC
D = 4096
FO = 512
EPS = 1e-6
SCALE = 128 ** -0.5
PI = float(np.pi)
E_V, E_X1, E_X2, E_HG, E_Q, E_K, E_AG = 0, 256, 512, 768, 1024, 1280, 1408
E_FM = 1664
E_NV = 128
O_Q, O_K, O_G = 0, 512, 1024
O_FM = 1536
O_NV = 512


class T:
    pass


def token_chunks():
    ch = [(c * 512, 512) for c in range(16)]
    ch.append((L, LC))
    return ch


class K:
    def __init__(self, nc, depth=4, dbg=None):
        self.nc = nc
        self.S = Sched(nc)
        self.depth = depth
        self.dbg = dbg or {}
        self.ncc = 0
        self.use_fft = True
        self.S.extra_sems = {}
        self.P2 = [nc.alloc_psum_tensor(f"psq{i}", [128, 1024], F32) for i in range(4)]
        self.ps = [self.P2[i // 2][:, (i % 2) * 512:(i % 2) * 512 + 512] for i in range(8)]
        self.psb = [Buf() for _ in range(8)]
        self.t = T()
        self.alloc_dram()

    def din(self, name, shape, dt=F32):
        return self.nc.dram_tensor(name, list(shape), dt, kind="ExternalInput").ap()

    def dscr(self, name, shape, dt=F32):
        return self.nc.dram_tensor(name, list(shape), dt).ap()

    def alloc_dram(self):
        t = self.t
        t.xT = self.din("xT", [FO, NT])
        t.cvT = self.din("cvT", [128, 2, 32])
        t.normg = self.din("normg", [128, 4, 4])
        t.adaw = self.din("adaw", [4 * D, 3 * FO])
        t.adab = self.din("adab", [128, 4, 3, 4])
        t.win = [self.din(f"win{l}", [D, (E_FM + E_NV) if l % 2 == 0 else (O_FM + O_NV)]) for l in range(4)]
        t.wout = [self.din(f"wout{l}", [D, FO]) for l in range(4)]
        t.hyconv = [self.din(f"hyconv{i}", [128, 2, 3, 3]) for i in range(2)]
        t.hyw1 = [self.din(f"hyw1_{i}", [33, 64]) for i in range(2)]
        t.hyw2 = [self.din(f"hyw2_{i}", [64, 64]) for i in range(2)]
        t.hyvec = [self.din(f"hyvec{i}", [64, 3]) for i in range(2)]
        t.hyw3 = [self.din(f"hyw3_{i}", [64, 1024]) for i in range(2)]
        t.hydd = [self.din(f"hydd{i}", [128, 2, 2, 2]) for i in range(2)]
        t.qkn = self.din("qkn", [128, 4, 2])
        t.btab = [self.din(f"btab{i}", [4, 3, 8, 128, 512]) for i in range(2)]
        t.zT = self.din("zT", [4, 33, L])
        t.tlin = self.din("tlin", [4, L])
        t.ropeC = self.din("ropeC", [128, L])
        t.ropeS = self.din("ropeS", [128, L])
        t.perm = self.din("perm", [128, 128])
        t.dft = self.din("dft", [128, 1536])
        t.out = self.nc.dram_tensor("outT", [FO, L], F32, kind="ExternalOutput").ap()
        t.xs = self.dscr("xs", [FO, NT])
        t.ssl = self.dscr("ssl", [1, NT])
        t.ssa = self.dscr("ssa", [8, NT])
        t.hTl = self.dscr("hTl", [FO, NT], BF16)
        t.hT = self.dscr("hT", [D, NT], BF16)
        t.uT = self.dscr("uT", [E_FM, NT])
        t.uV = self.dscr("uV", [NT, 512], BF16)
        t.hd2 = self.dscr("hd2", [4, 64, L])
        t.zsc = self.dscr("zsc", [128, L], BF16)
        t.fsc = self.dscr("fsc", [2, 128, L], BF16)
        t.ysc = self.dscr("ysc", [128, L])
        t.gTl = self.dscr("gTl", [FO, NT], BF16)
        t.gT = self.dscr("gT", [D, NT], BF16)

    def allgather(self, src, dst):
        S = self.S
        S.barrier()
        self.ncc += 1
        n = self.ncc
        S.custom("pool", lambda e: e.collective_compute("AllGather", ALU.bypass, replica_groups=[list(range(NCORES))],
                                                         ins=[src], outs=[dst]), "cc", 1, n)
        S.extra_sems["cc"] = n
        S.barrier()

    def consts(self):
        S = self.S
        self.ones_f = S.sb([128, 128], F32, "ones")
        self.ones_b = S.sb([128, 128], BF16, "onesb")
        self.bconst = Buf()
        S.op("pool", lambda e: e.memset(self.ones_f[:], 1.0), writes=[self.bconst])
        S.op("pool", lambda e: e.memset(self.ones_b[:], 1.0), writes=[self.bconst])
        self.perm = S.sb([128, 128], F32, "perm")
        S.dma(self.perm[:], self.t.perm[:, :], writes=[self.bconst])
        self.mods = S.sb([128, 4, 3, 4, 2], F32, "mods")
        self.bmods = Buf()
        self.AB = S.sb([128, 4, 2, 4, 2], F32, "AB")
        self.qkn = S.sb([128, 4, 2], F32, "qkn")
        S.dma(self.qkn[:], self.t.qkn[:, :, :], writes=[self.bconst])
        dftf = S.sb([128, 1536], F32, "dftf")
        S.dma(dftf[:], self.t.dft[:, :], writes=[self.bconst])
        self.TWre = dftf[:, 256:384]
        self.TWim = dftf[:, 384:512]
        self.dftb = S.sb([128, 1536], BF16, "dftb")
        S.op("dve", lambda e: e.tensor_copy(out=self.dftb[:], in_=dftf[:]), reads=[self.bconst], writes=[self.bconst])
        db = self.dftb
        self.F1cat = db[0:64, 0:256]
        self.F2re, self.F2im, self.F2imn = db[:, 512:640], db[:, 640:768], db[:, 768:896]
        self.R1, self.R2 = db[:, 896:1152], db[:, 1152:1408]
        self.G1re, self.G1im = db[:, 1408:1472], db[:, 1472:1536]
        self.base_mark = S.mark()

    def stage_ada(self):
        S, t = self.S, self.t
        m = S.mark()
        cv = S.sb([128, 2, 32], F32)
        cs = S.sb([128, 32, 2], F32)
        bcv = Buf()
        S.dma(cv[:], t.cvT[:, :, :], writes=[bcv])
        S.op("act", lambda e: e.activation(out=cs[:].rearrange("p k v -> p v k"), in_=cv[:], func=AF.Silu), reads=[bcv], writes=[bcv])
        ab = S.sb([128, 4, 3, 4], F32)
        ng = S.sb([128, 4, 4], F32)
        bab = Buf()
        S.dma(ab[:], t.adab[:, :, :, :], writes=[bab])
        S.dma(ng[:], t.normg[:, :, :], writes=[bab])
        wb = [S.sb([128, 3 * FO], F32) for _ in range(3)]
        wbb = [Buf() for _ in range(3)]
        nw = 0
        for layer in range(self.depth):
            for half in range(2):
                for kc in range(32):
                    i = nw % 3
                    nw += 1
                    r0 = layer * D + kc * 128
                    S.dma(wb[i][:], t.adaw[r0:r0 + 128, :], writes=[wbb[i]])
                    for ch in range(3):
                        for f2 in range(2):
                            fc = half * 2 + f2
                            g = ch * 2 + f2
                            S.op("pe", lambda e, g=g, w=wb[i][:, ch * FO + fc * 128: ch * FO + fc * 128 + 128], r=cs[:, kc, :], kc=kc:
                                 e.matmul(self.ps[g][:, 0:2], lhsT=w, rhs=r, start=(kc == 0), stop=(kc == 31)),
                                 reads=[wbb[i], bcv], writes=[self.psb[g]])
                for ch in range(3):
                    for f2 in range(2):
                        fc = half * 2 + f2
                        g = ch * 2 + f2
                        S.op("dve", lambda e, g=g, d=self.mods[:, layer, ch, fc, :], b=ab[:, layer, ch, fc:fc + 1]:
                             e.tensor_scalar_add(out=d, in0=self.ps[g][:, 0:2], scalar1=b), reads=[self.psb[g], bab], writes=[self.bmods])
            for v in range(2):
                S.op("dve", lambda e, layer=layer, v=v: e.scalar_tensor_tensor(
                    out=self.AB[:, layer, 0, :, v], in0=self.mods[:, layer, 1, :, v], scalar=1.0, in1=ng[:, layer, :],
                    op0=ALU.add, op1=ALU.mult), reads=[self.bmods, bab], writes=[self.bmods])
                S.op("dve", lambda e, layer=layer, v=v: e.tensor_copy(out=self.AB[:, layer, 1, :, v], in_=self.mods[:, layer, 0, :, v]),
                     reads=[self.bmods], writes=[self.bmods])
        S.barrier()
        S.release(m)

    def stage_norm(self, layer, src):
        S, t = self.S, self.t
        m = S.mark()
        chunks = token_chunks()
        xb = [S.sb([128, 4, 512], F32) for _ in range(2)]
        xbb = [Buf() for _ in range(2)]
        sq = S.sb([128, 4, 512], F32)
        bsq = Buf()
        row = [S.sb([1, 512], F32) for _ in range(2)]
        rowb = [Buf() for _ in range(2)]
        for ci, (c0, n) in enumerate(chunks):
            i = ci % 2
            S.dma(xb[i][:, :, :n], src.rearrange("(fc p) t -> p fc t", p=128)[:, :, c0:c0 + n], writes=[xbb[i]])
            S.op("act", lambda e, i=i, n=n: e.activation(out=sq[:, :, :n], in_=xb[i][:, :, :n], func=AF.Square), reads=[xbb[i]], writes=[bsq])
            pj = ci % 2
            for fc in range(4):
                S.op("pe", lambda e, fc=fc, n=n, pj=pj: e.matmul(self.ps[pj][:, :n], lhsT=self.ones_f[:], rhs=sq[:, fc, :n], start=(fc == 0), stop=(fc == 3)),
                     reads=[bsq, self.bconst], writes=[self.psb[pj]])
            S.op("dve", lambda e, i=i, n=n, pj=pj: e.tensor_copy(out=row[i][:, :n], in_=self.ps[pj][0:1, :n]), reads=[self.psb[pj]], writes=[rowb[i]])
            S.dma(t.ssl[:, c0:c0 + n], row[i][:, :n], reads=[rowb[i]])
        self.allgather(t.ssl[:, :], t.ssa[:, :])
        s8 = [S.sb([8, 512], F32) for _ in range(2)]
        s8b = [Buf() for _ in range(2)]
        rstd = S.sb([128, 512], F32)
        brs = Buf()
        hb = [S.sb([128, 4, 512], BF16) for _ in range(2)]
        hbb = [Buf() for _ in range(2)]
        tmp = S.sb([128, 512], F32)
        btmp = Buf()
        for ci, (c0, n) in enumerate(chunks):
            i = ci % 2
            v = 1 if c0 >= L else 0
            S.dma(xb[i][:, :, :n], src.rearrange("(fc p) t -> p fc t", p=128)[:, :, c0:c0 + n], writes=[xbb[i]])
            S.dma(s8[i][:, :n], t.ssa[:, c0:c0 + n], writes=[s8b[i]])
            pj = 2 + ci % 2
            S.op("pe", lambda e, i=i, n=n, pj=pj: e.matmul(self.ps[pj][:, :n], lhsT=self.ones_f[0:8, :], rhs=s8[i][:, :n], start=True, stop=True),
                 reads=[s8b[i], self.bconst], writes=[self.psb[pj]])
            S.op("dve", lambda e, n=n, pj=pj: e.tensor_scalar(out=rstd[:, :n], in0=self.ps[pj][:, :n], scalar1=1.0 / D, scalar2=EPS, op0=ALU.mult, op1=ALU.add),
                 reads=[self.psb[pj]], writes=[brs])
            S.op("act", lambda e, n=n: e.activation(out=rstd[:, :n], in_=rstd[:, :n], func=AF.Sqrt), reads=[brs], writes=[brs])
            S.op("dve", lambda e, n=n: e.reciprocal(out=rstd[:, :n], in_=rstd[:, :n]), reads=[brs], writes=[brs])
            for fc in range(4):
                S.op("dve", lambda e, i=i, n=n, fc=fc: e.tensor_tensor(out=tmp[:, :n], in0=xb[i][:, fc, :n], in1=rstd[:, :n], op=ALU.mult),
                     reads=[xbb[i], brs], writes=[btmp])
                S.op("dve", lambda e, i=i, n=n, fc=fc, v=v: e.tensor_scalar(out=hb[i][:, fc, :n], in0=tmp[:, :n],
                                                                          scalar1=self.AB[:, layer, 0, fc, v:v + 1], scalar2=self.AB[:, layer, 1, fc, v:v + 1],
                                                                          op0=ALU.mult, op1=ALU.add),
                     reads=[btmp, self.bmods], writes=[hbb[i]])
            S.dma(t.hTl.rearrange("(fc p) t -> p fc t", p=128)[:, :, c0:c0 + n], hb[i][:, :, :n], reads=[hbb[i]])
        self.allgather(t.hTl[:, :], t.hT[:, :])
        S.release(m)

    def gemm(self, W, ncols_fm, src, epilogue, ncols_tm=0, epilogue_tm=None, chunks=None):
        S = self.S
        m = S.mark()
        chunks = chunks or token_chunks()
        nblk = ncols_fm // 128
        groups = [list(range(g, min(g + 4, nblk))) for g in range(0, nblk, 4)]
        if ncols_tm:
            groups.append("tm")
        wg = S.sb([128, 32, 512], BF16, "wg")
        bwg = Buf()
        wst = [S.sb([128, 512], F32, "wst") for _ in range(3)]
        wstb = [Buf() for _ in range(3)]
        hx = [S.sb([128, 32, 512], BF16, "hx") for _ in range(2)]
        hxb = [Buf() for _ in range(2)]
        nld = 0
        pk = 0
        for g in groups:
            if g == "tm":
                c0w, ncw = ncols_fm, ncols_tm
            else:
                c0w, ncw = g[0] * 128, len(g) * 128
            for kc in range(32):
                i = kc % 3
                S.dma(wst[i][:, :ncw], W[kc * 128:(kc + 1) * 128, c0w:c0w + ncw], writes=[wstb[i]])
                S.op("pool", lambda e, i=i, kc=kc, ncw=ncw: e.tensor_copy(out=wg[:, kc, :ncw], in_=wst[i][:, :ncw]), reads=[wstb[i]], writes=[bwg])
            for (t0, n) in chunks:
                i = nld % 2
                nld += 1
                for q4 in range(4):
                    S.dma(hx[i][:, q4 * 8:(q4 + 1) * 8, :n], src.rearrange("(kc p) t -> p kc t", p=128)[:, q4 * 8:(q4 + 1) * 8, t0:t0 + n], writes=[hxb[i]])
                if g == "tm":
                    for sub in range(n // 128):
                        pj = 4 + pk % 4
                        pk += 1
                        for kc in range(32):
                            S.op("pe", lambda e, i=i, kc=kc, sub=sub, pj=pj, ncw=ncw: e.matmul(
                                self.ps[pj][:, :ncw], lhsT=hx[i][:, kc, sub * 128:(sub + 1) * 128], rhs=wg[:, kc, :ncw],
                                start=(kc == 0), stop=(kc == 31)), reads=[hxb[i], bwg], writes=[self.psb[pj]])
                        epilogue_tm(self.ps[pj][:, :ncw], self.psb[pj], t0 + sub * 128)
                else:
                    for bi, cb in enumerate(g):
                        pj = 4 + pk % 4
                        pk += 1
                        for kc in range(32):
                            S.op("pe", lambda e, i=i, kc=kc, bi=bi, pj=pj, n=n: e.matmul(
                                self.ps[pj][:, :n], lhsT=wg[:, kc, bi * 128:(bi + 1) * 128], rhs=hx[i][:, kc, :n],
                                start=(kc == 0), stop=(kc == 31)), reads=[hxb[i], bwg], writes=[self.psb[pj]])
                        epilogue(self.ps[pj][:, :n], self.psb[pj], cb, t0, n)
        S.barrier()
        S.release(m)

    def stage_proj(self, layer):
        S, t = self.S, self.t
        even = layer % 2 == 0
        m = S.mark()
        st = [S.sb([128, 512], F32, "pst") for _ in range(4)]
        stb = [Buf() for _ in range(4)]
        sv = [S.sb([128, 512], BF16, "psv") for _ in range(2)]
        svb = [Buf() for _ in range(2)]
        cnt = [0, 0]

        def epi(ps, psb, cb, t0, n):
            i = cnt[0] % 4
            cnt[0] += 1
            eng = "act" if i % 2 == 0 else "dve"
            if eng == "act":
                S.op("act", lambda e, i=i, n=n: e.activation(out=st[i][:, :n], in_=ps, func=AF.Copy), reads=[psb], writes=[stb[i]])
            else:
                S.op("dve", lambda e, i=i, n=n: e.tensor_copy(out=st[i][:, :n], in_=ps), reads=[psb], writes=[stb[i]])
            S.dma(t.uT[cb * 128:(cb + 1) * 128, t0:t0 + n], st[i][:, :n], reads=[stb[i]])

        def epi_tm(ps, psb, tok0):
            i = cnt[1] % 2
            cnt[1] += 1
            ncw = E_NV if even else O_NV
            S.op("act", lambda e, i=i: e.activation(out=sv[i][:, :ncw], in_=ps, func=AF.Copy), reads=[psb], writes=[svb[i]])
            S.dma(t.uV[tok0:tok0 + 128, 0:ncw], sv[i][:, :ncw], reads=[svb[i]])

        self.gemm(t.win[layer], E_FM if even else O_FM, t.hT, epi, E_NV if even else O_NV, epi_tm)
        S.release(m)

    def stage_out(self, layer, src, dst, last):
        S, t = self.S, self.t
        m = S.mark()
        xin = [S.sb([128, 512], F32, "oxi") for _ in range(3)]
        xinb = [Buf() for _ in range(3)]
        cnt = [0]

        def epi(ps, psb, cb, t0, n):
            i = cnt[0] % 3
            cnt[0] += 1
            v = 1 if t0 >= L else 0
            S.dma(xin[i][:, :n], src[cb * 128:(cb + 1) * 128, t0:t0 + n], writes=[xinb[i]])
            S.op("dve", lambda e, i=i, n=n, cb=cb, v=v: e.scalar_tensor_tensor(
                out=xin[i][:, :n], in0=ps, scalar=self.mods[:, layer, 2, cb, v:v + 1], in1=xin[i][:, :n], op0=ALU.mult, op1=ALU.add),
                reads=[psb, xinb[i], self.bmods], writes=[xinb[i]])
            if last:
                S.dma(t.out[cb * 128:(cb + 1) * 128, t0:t0 + n], xin[i][:, :n], reads=[xinb[i]])
            else:
                S.dma(dst[cb * 128:(cb + 1) * 128, t0:t0 + n], xin[i][:, :n], reads=[xinb[i]])

        chunks = token_chunks()
        if last:
            chunks = chunks[:-1]
        self.gemm(t.wout[layer], FO, t.gT, epi, chunks=chunks)
        S.release(m)

    def sin_layer(self, ps, psb, freq, fbcol, out, outb, n, a, tm, ba):
        S = self.S
        S.op("dve", lambda e: e.tensor_scalar(out=a[:, :n], in0=ps, scalar1=freq, scalar2=fbcol, op0=ALU.mult, op1=ALU.add), reads=[psb, self.bhyw], writes=[ba])
        S.op("dve", lambda e: e.tensor_scalar(out=tm[:, :n], in0=a[:, :n], scalar1=PI, scalar2=-2 * PI, op0=ALU.is_gt, op1=ALU.mult), reads=[ba], writes=[ba])
        S.op("dve", lambda e: e.tensor_tensor(out=a[:, :n], in0=a[:, :n], in1=tm[:, :n], op=ALU.add), reads=[ba], writes=[ba])
        S.op("dve", lambda e: e.tensor_scalar(out=tm[:, :n], in0=a[:, :n], scalar1=-PI, scalar2=2 * PI, op0=ALU.is_lt, op1=ALU.mult), reads=[ba], writes=[ba])
        S.op("dve", lambda e: e.tensor_tensor(out=a[:, :n], in0=a[:, :n], in1=tm[:, :n], op=ALU.add), reads=[ba], writes=[ba])
        S.op("act", lambda e: e.activation(out=out, in_=a[:, :n], func=AF.Sin), reads=[ba], writes=[outb])

    def stage_hyena(self, layer, need_ctx):
        S, t = self.S, self.t
        i = layer // 2
        m = S.mark()
        w1 = S.sb([33, 64], F32)
        w2 = S.sb([64, 64], F32)
        vec = S.sb([64, 3], F32)
        fb = S.sb([64, 2], F32)
        w3 = S.sb([64, 1024], F32)
        cw = S.sb([128, 2, 3, 3], F32)
        dd = S.sb([128, 2, 2, 2], F32)
        nd = S.sb([128, 2, 2], F32)
        self.bhyw = bw = Buf()
        S.dma(w1[:], t.hyw1[i][:, :], writes=[bw])
        S.dma(w2[:], t.hyw2[i][:, :], writes=[bw])
        S.dma(vec[:], t.hyvec[i][:, :], writes=[bw])
        S.dma(w3[:], t.hyw3[i][:, :], writes=[bw])
        S.dma(cw[:], t.hyconv[i][:, :, :, :], writes=[bw])
        S.dma(dd[:], t.hydd[i][:, :, :, :], writes=[bw])
        S.op("dve", lambda e: e.tensor_tensor(out=fb[:, 0:1], in0=vec[:, 0:1], in1=vec[:, 2:3], op=ALU.mult), reads=[bw], writes=[bw])
        S.op("dve", lambda e: e.tensor_tensor(out=fb[:, 1:2], in0=vec[:, 1:2], in1=vec[:, 2:3], op=ALU.mult), reads=[bw], writes=[bw])
        S.op("act", lambda e: e.activation(out=nd[:], in_=dd[:, :, 0, :], func=AF.Abs), reads=[bw], writes=[bw])
        S.op("dve", lambda e: e.tensor_scalar_mul(out=nd[:], in0=nd[:], scalar1=-1.0), reads=[bw], writes=[bw])
        m2 = S.mark()
        zt = [S.sb([33, 512], F32) for _ in range(2)]
        ztb = [Buf() for _ in range(2)]
        h1 = S.sb([64, 512], F32)
        h2 = [S.sb([64, 512], F32) for _ in range(2)]
        h1b = Buf()
        h2b = [Buf() for _ in range(2)]
        a = S.sb([64, 512], F32)
        tm = S.sb([64, 512], F32)
        ba = Buf()
        k = 0
        for vi in range(4):
            Ls = L if vi < 2 else LC
            if vi >= 2 and not need_ctx:
                continue
            for c0 in range(0, Ls, 512):
                n = min(512, Ls - c0)
                j = k % 2
                k += 1
                S.dma(zt[j][:, :n], t.zT[vi, :, c0:c0 + n], writes=[ztb[j]])
                S.op("pe", lambda e, j=j, n=n: e.matmul(self.ps[0][0:64, :n], lhsT=w1[:], rhs=zt[j][:, :n], start=True, stop=True), reads=[bw, ztb[j]], writes=[self.psb[0]])
                self.sin_layer(self.ps[0][0:64, :n], self.psb[0], vec[:, 2:3], fb[:, 0:1], h1[:, :n], h1b, n, a, tm, ba)
                S.op("pe", lambda e, n=n: e.matmul(self.ps[1][0:64, :n], lhsT=w2[:], rhs=h1[:, :n], start=True, stop=True), reads=[bw, h1b], writes=[self.psb[1]])
                self.sin_layer(self.ps[1][0:64, :n], self.psb[1], vec[:, 2:3], fb[:, 1:2], h2[j][:, :n], h2b[j], n, a, tm, ba)
                S.dma(t.hd2[vi, :, c0:c0 + n], h2[j][:, :n], reads=[h2b[j]])
        S.barrier()
        S.release(m2)
        z = S.sb([128, L], F32, "hz")
        hk2 = S.sb([128, 2 * L if not self.use_fft else 512], F32, "hk2")
        acc = S.sb([128, L], F32, "hacc")
        bz, bhk, bacc = Buf(), Buf(), Buf()
        raw = [S.sb([128, 514], F32) for _ in range(2)]
        rawb = [Buf() for _ in range(2)]
        xc = S.sb([128, 512], F32)
        bxc = Buf()
        tmp = S.sb([128, 512], F32)
        btmp = Buf()
        ob = [S.sb([128, 512], BF16) for _ in range(2)]
        obb = [Buf() for _ in range(2)]
        hch = [S.sb([64, 512], F32) for _ in range(2)]
        hchb = [Buf() for _ in range(2)]
        tl = [S.sb([128, 512], F32) for _ in range(2)]
        tlb = [Buf() for _ in range(2)]
        win = S.sb([128, 512], F32)
        bwin = Buf()
        cnt = {"raw": 0, "h": 0, "ob": 0}

        def conv_chunk(blk, kind, row0, col0, Ls, c0, n, out, outb):
            j = cnt["raw"] % 2
            cnt["raw"] += 1
            lo, hi = c0 - 1, c0 + n + 1
            slo, shi = max(lo, 0), min(hi, Ls)
            S.op("pool", lambda e, j=j: e.memset(raw[j][:, :], 0.0), writes=[rawb[j]])
            S.dma(raw[j][:, slo - lo:shi - lo], t.uT[row0 + blk * 128:row0 + blk * 128 + 128, col0 + slo:col0 + shi], writes=[rawb[j]])
            S.op("dve", lambda e, j=j: e.tensor_scalar_mul(out=out, in0=raw[j][:, 1:n + 1], scalar1=cw[:, blk, kind, 1:2]), reads=[rawb[j], bw], writes=[outb])
            S.op("dve", lambda e, j=j: e.scalar_tensor_tensor(out=out, in0=raw[j][:, 0:n], scalar=cw[:, blk, kind, 0:1], in1=out, op0=ALU.mult, op1=ALU.add), reads=[rawb[j], bw], writes=[outb])
            S.op("dve", lambda e, j=j: e.scalar_tensor_tensor(out=out, in0=raw[j][:, 2:n + 2], scalar=cw[:, blk, kind, 2:3], in1=out, op0=ALU.mult, op1=ALU.add), reads=[rawb[j], bw], writes=[outb])

        seqs = [(0, L, 0, 1)]
        if need_ctx:
            seqs.append((L, LC, 2, 3))
        zb = [S.sb([128, 2048], BF16) for _ in range(2)]
        zbb = [Buf() for _ in range(2)]
        fst = [S.sb([128, 512], BF16) for _ in range(2)]
        fstb = [Buf() for _ in range(2)]
        xin = [[S.sb([64, 16, 128], BF16) for _ in range(3)] for _ in range(2)]
        xinb = [[Buf() for _ in range(3)] for _ in range(2)]
        ysb = [S.sb([64, 16, 128], F32) for _ in range(2)]
        ysbb = [Buf() for _ in range(2)]
        tq = [[S.sb([128, 4, 128], F32) for _ in range(4)] for _ in range(2)]
        tqb = [[Buf() for _ in range(4)] for _ in range(2)]
        Bq = [[S.sb([128, 4, 128], BF16) for _ in range(2)] for _ in range(2)]
        Bqb = [[Buf() for _ in range(2)] for _ in range(2)]
        Hre = S.sb([128, 4, 128], F32)
        Him = S.sb([128, 4, 128], F32)
        bH = Buf()
        bzsc, bfsc, bysc = Buf(), Buf(), Buf()
        fc = {"k": 0, "zb": 0, "f": 0, "sb": 0}
        TWr = self.TWre.unsqueeze(1).broadcast_to([128, 4, 128])
        TWi = self.TWim.unsqueeze(1).broadcast_to([128, 4, 128])

        def cmul(are, aim, bre, bim, rd, conj, outre, outim, outb_re, outb_im):
            k = fc["k"] % 2
            fc["k"] += 1
            t1, t2, t3, t4 = tq[k]
            b1, b2, b3, b4 = tqb[k]
            S.op("dve", lambda e: e.tensor_tensor(out=t1[:], in0=are, in1=bre, op=ALU.mult), reads=rd, writes=[b1])
            S.op("dve", lambda e: e.tensor_tensor(out=t2[:], in0=aim, in1=bim, op=ALU.mult), reads=rd, writes=[b2])
            S.op("dve", lambda e: e.tensor_tensor(out=t3[:], in0=are, in1=bim, op=ALU.mult), reads=rd, writes=[b3])
            S.op("dve", lambda e: e.tensor_tensor(out=t4[:], in0=aim, in1=bre, op=ALU.mult), reads=rd, writes=[b4])
            if not conj:
                S.op("pool", lambda e: e.tensor_tensor(out=outre, in0=t1[:], in1=t2[:], op=ALU.subtract), reads=[b1, b2], writes=[outb_re])
                S.op("pool", lambda e: e.tensor_tensor(out=outim, in0=t3[:], in1=t4[:], op=ALU.add), reads=[b3, b4], writes=[outb_im])
            else:
                S.op("pool", lambda e: e.tensor_tensor(out=outre, in0=t1[:], in1=t2[:], op=ALU.add), reads=[b1, b2], writes=[outb_re])
                S.op("pool", lambda e: e.tensor_tensor(out=outim, in0=t4[:], in1=t3[:], op=ALU.subtract), reads=[b3, b4], writes=[outb_im])

        def fft_fwd(src, srcb, g):
            k = fc["f"] % 2
            fc["f"] += 1
            PAC = self.P2[0] if k == 0 else self.P2[2]
            pb = [self.psb[0], self.psb[1]] if k == 0 else [self.psb[4], self.psb[5]]
            for c4 in range(4):
                S.op("pe", lambda e, c4=c4: e.matmul(PAC[:, c4 * 256:(c4 + 1) * 256], lhsT=src[:, g * 4 + c4, :], rhs=self.F1cat, start=True, stop=True),
                     reads=[srcb, self.bconst], writes=pb)
            Av = PAC[:, :].rearrange("p (c r k) -> p c r k", c=4, r=2)
            bre, bim = Bq[k]
            bbre, bbim = Bqb[k]
            cmul(Av[:, :, 0, :], Av[:, :, 1, :], TWr, TWi, pb + [self.bconst], False, bre[:], bim[:], bbre, bbim)
            br2 = bre[:].rearrange("p c k -> p (c k)")
            bi2 = bim[:].rearrange("p c k -> p (c k)")
            S.op("pe", lambda e: e.matmul(self.ps[2], lhsT=self.F2re, rhs=br2, start=True, stop=False), reads=[bbre, self.bconst], writes=[self.psb[2]])
            S.op("pe", lambda e: e.matmul(self.ps[2], lhsT=self.F2imn, rhs=bi2, start=False, stop=True), reads=[bbim, self.bconst], writes=[self.psb[2]])
            S.op("pe", lambda e: e.matmul(self.ps[3], lhsT=self.F2im, rhs=br2, start=True, stop=False), reads=[bbre, self.bconst], writes=[self.psb[3]])
            S.op("pe", lambda e: e.matmul(self.ps[3], lhsT=self.F2re, rhs=bi2, start=False, stop=True), reads=[bbim, self.bconst], writes=[self.psb[3]])

        def fft_group(q, g):
            Xre = self.ps[2].rearrange("p (c k) -> p c k", c=4)
            Xim = self.ps[3].rearrange("p (c k) -> p c k", c=4)
            fft_fwd(xin[q][1], xinb[q][1], g)
            S.op("act", lambda e: e.activation(out=Hre[:], in_=Xre, func=AF.Copy), reads=[self.psb[2]], writes=[bH])
            S.op("act", lambda e: e.activation(out=Him[:], in_=Xim, func=AF.Copy), reads=[self.psb[3]], writes=[bH])
            fft_fwd(xin[q][2], xinb[q][2], g)
            S.op("dve", lambda e: e.tensor_tensor(out=Hre[:], in0=Hre[:], in1=Xre, op=ALU.add), reads=[self.psb[2], bH], writes=[bH])
            S.op("dve", lambda e: e.tensor_tensor(out=Him[:], in0=Him[:], in1=Xim, op=ALU.subtract), reads=[self.psb[3], bH], writes=[bH])
            fft_fwd(xin[q][0], xinb[q][0], g)
            k = fc["f"] % 2
            fc["f"] += 1
            yre, yim = Bq[k]
            ybre, ybim = Bqb[k]
            cmul(Xre, Xim, Hre[:], Him[:], [self.psb[2], self.psb[3], bH], False, yre[:], yim[:], ybre, ybim)
            PAC = self.P2[0] if k == 0 else self.P2[2]
            pb = [self.psb[0], self.psb[1]] if k == 0 else [self.psb[4], self.psb[5]]
            for c4 in range(4):
                S.op("pe", lambda e, c4=c4: e.matmul(PAC[:, c4 * 256:(c4 + 1) * 256], lhsT=yre[:, c4, :], rhs=self.R1, start=True, stop=False),
                     reads=[ybre, self.bconst], writes=pb)
                S.op("pe", lambda e, c4=c4: e.matmul(PAC[:, c4 * 256:(c4 + 1) * 256], lhsT=yim[:, c4, :], rhs=self.R2, start=False, stop=True),
                     reads=[ybim, self.bconst], writes=pb)
            Cv = PAC[:, :].rearrange("p (c r k) -> p c r k", c=4, r=2)
            k2 = fc["f"] % 2
            fc["f"] += 1
            dre, dim = Bq[k2]
            dbre, dbim = Bqb[k2]
            cmul(Cv[:, :, 0, :], Cv[:, :, 1, :], TWr, TWi, pb + [self.bconst], True, dre[:], dim[:], dbre, dbim)
            S.op("pe", lambda e: e.matmul(self.ps[6][0:64, :], lhsT=self.G1re, rhs=dre[:].rearrange("p c k -> p (c k)"), start=True, stop=False),
                 reads=[dbre, self.bconst], writes=[self.psb[6]])
            S.op("pe", lambda e: e.matmul(self.ps[6][0:64, :], lhsT=self.G1im, rhs=dim[:].rearrange("p c k -> p (c k)"), start=False, stop=True),
                 reads=[dbim, self.bconst], writes=[self.psb[6]])
            S.op("act", lambda e: e.activation(out=ysb[q][:, g * 4:(g + 1) * 4, :], in_=self.ps[6][0:64, :].rearrange("p (c k) -> p c k", c=4), func=AF.Copy),
                 reads=[self.psb[6]], writes=[ysbb[q]])

        def fft_conv(blk, o, vf):
            for c0 in range(0, L, 2048):
                q = fc["zb"] % 2
                fc["zb"] += 1
                S.op("act", lambda e, q=q, c0=c0: e.activation(out=zb[q][:], in_=z[:, c0:c0 + 2048], func=AF.Copy), reads=[bz], writes=[zbb[q]])
                S.dma(t.zsc[:, c0:c0 + 2048], zb[q][:], reads=[zbb[q]], writes=[bzsc])
            for dirn in range(2):
                for c0 in range(0, L, 512):
                    j = cnt["h"] % 2
                    cnt["h"] += 1
                    S.dma(hch[j][:, :], t.hd2[vf, :, c0:c0 + 512], writes=[hchb[j]])
                    S.dma(tl[j][:, :], t.tlin[vf, c0:c0 + 512].partition_broadcast(128), writes=[tlb[j]])
                    wc = (o * 2 + dirn) * 256 + blk * 128
                    S.op("pe", lambda e, j=j, wc=wc: e.matmul(self.ps[6 + j], lhsT=w3[:, wc:wc + 128], rhs=hch[j][:, :], start=True, stop=True),
                         reads=[bw, hchb[j]], writes=[self.psb[6 + j]])
                    S.op("act", lambda e, j=j: e.activation(out=win[:, :], in_=tl[j][:, :], func=AF.Exp, scale=nd[:, blk, o:o + 1]), reads=[tlb[j], bw], writes=[bwin])
                    S.op("dve", lambda e, j=j: e.tensor_tensor(out=fst[j][:, :], in0=self.ps[6 + j], in1=win[:, :], op=ALU.mult), reads=[self.psb[6 + j], bwin], writes=[fstb[j]])
                    if dirn == 1 and c0 == 0:
                        S.op("dve", lambda e, j=j: e.memset(fst[j][:, 0:1], 0.0), reads=[fstb[j]], writes=[fstb[j]])
                    S.dma(t.fsc[dirn, :, c0:c0 + 512], fst[j][:, :], reads=[fstb[j]], writes=[bfsc])
            for sbk in range(8):
                q = fc["sb"] % 2
                fc["sb"] += 1
                rows = slice(sbk * 16, sbk * 16 + 16)
                S.dma(xin[q][0][:], t.zsc[rows, :].rearrange("c (a b) -> a c b", b=128), reads=[bzsc], writes=[xinb[q][0]])
                S.dma(xin[q][1][:], t.fsc[0, rows, :].rearrange("c (a b) -> a c b", b=128), reads=[bfsc], writes=[xinb[q][1]])
                S.dma(xin[q][2][:], t.fsc[1, rows, :].rearrange("c (a b) -> a c b", b=128), reads=[bfsc], writes=[xinb[q][2]])
                for g in range(4):
                    fft_group(q, g)
                S.dma(t.ysc[rows, :].rearrange("c (a b) -> a c b", b=128), ysb[q][:], reads=[ysbb[q]], writes=[bysc])
            S.dma(acc[:, :], t.ysc[:, :], reads=[bysc], writes=[bacc])

        def do_seq(blk, col0, Ls, vf, vr):
            chunks = [(c, min(512, Ls - c)) for c in range(0, Ls, 512)]
            for (c0, n) in chunks:
                conv_chunk(blk, 0, E_V, col0, Ls, c0, n, z[:, c0:c0 + n], bz)
            def direct_conv(o):
                for (dirn, vi, off) in ((1, vr, 0), (0, vf, Ls - 1)):
                    for (c0, n) in chunks:
                        j = cnt["h"] % 2
                        cnt["h"] += 1
                        S.dma(hch[j][:, :n], t.hd2[vi, :, c0:c0 + n], writes=[hchb[j]])
                        S.dma(tl[j][:, :n], t.tlin[vi, c0:c0 + n].partition_broadcast(128), writes=[tlb[j]])
                        wc = (o * 2 + dirn) * 256 + blk * 128
                        S.op("pe", lambda e, j=j, n=n, wc=wc: e.matmul(self.ps[2 + j][:, :n], lhsT=w3[:, wc:wc + 128], rhs=hch[j][:, :n], start=True, stop=True),
                             reads=[bw, hchb[j]], writes=[self.psb[2 + j]])
                        S.op("act", lambda e, j=j, n=n, o=o: e.activation(out=win[:, :n], in_=tl[j][:, :n], func=AF.Exp, scale=nd[:, blk, o:o + 1]), reads=[tlb[j], bw], writes=[bwin])
                        S.op("dve", lambda e, j=j, n=n, off=off, c0=c0: e.tensor_tensor(out=hk2[:, off + c0:off + c0 + n], in0=self.ps[2 + j][:, :n], in1=win[:, :n], op=ALU.mult),
                             reads=[self.psb[2 + j], bwin], writes=[bhk])
                for s_ in range(Ls):
                    if s_ == 0:
                        S.op("dve", lambda e: e.tensor_scalar_mul(out=acc[:, :Ls], in0=hk2[:, Ls - 1:2 * Ls - 1], scalar1=z[:, 0:1]), reads=[bz, bhk], writes=[bacc])
                    else:
                        S.op("dve", lambda e, s_=s_: e.scalar_tensor_tensor(out=acc[:, :Ls], in0=hk2[:, Ls - 1 - s_:2 * Ls - 1 - s_], scalar=z[:, s_:s_ + 1], in1=acc[:, :Ls], op0=ALU.mult, op1=ALU.add),
                             reads=[bz, bhk], writes=[bacc])
            def do_order(o):
                if Ls == L and self.use_fft:
                    fft_conv(blk, o, vf)
                else:
                    direct_conv(o)
                for (c0, n) in chunks:
                    conv_chunk(blk, o + 1, E_X1 if o == 0 else E_X2, col0, Ls, c0, n, xc[:, :n], bxc)
                    S.op("dve", lambda e, c0=c0, n=n, o=o: e.scalar_tensor_tensor(out=tmp[:, :n], in0=z[:, c0:c0 + n], scalar=dd[:, blk, 1, o:o + 1], in1=acc[:, c0:c0 + n], op0=ALU.mult, op1=ALU.add),
                         reads=[bz, bacc, bw], writes=[btmp])
                    S.op("dve", lambda e, c0=c0, n=n: e.tensor_tensor(out=z[:, c0:c0 + n], in0=xc[:, :n], in1=tmp[:, :n], op=ALU.mult), reads=[bxc, btmp], writes=[bz])
            do_order(0)
            do_order(1)
            for (c0, n) in chunks:
                j = cnt["raw"] % 2
                cnt["raw"] += 1
                q = cnt["ob"] % 2
                cnt["ob"] += 1
                S.dma(raw[j][:, :n], t.uT[E_HG + blk * 128:E_HG + blk * 128 + 128, col0 + c0:col0 + c0 + n], writes=[rawb[j]])
                S.op("act", lambda e, j=j, n=n: e.activation(out=raw[j][:, :n], in_=raw[j][:, :n], func=AF.Silu), reads=[rawb[j]], writes=[rawb[j]])
                S.op("dve", lambda e, j=j, n=n, q=q, c0=c0: e.tensor_tensor(out=ob[q][:, :n], in0=z[:, c0:c0 + n], in1=raw[j][:, :n], op=ALU.mult), reads=[bz, rawb[j]], writes=[obb[q]])
                S.dma(t.gTl[blk * 128:(blk + 1) * 128, col0 + c0:col0 + c0 + n], ob[q][:, :n], reads=[obb[q]])

        for blk in range(2):
            for (col0, Ls, vf, vr) in seqs:
                do_seq(blk, col0, Ls, vf, vr)
        S.barrier()
        S.release(m)

    def qk_prep(self, row0, gain, dst, dstb, rope, with_ctx=True):
        S, t = self.S, self.t
        m = S.mark()
        raw = [S.sb([128, 512], F32) for _ in range(2)]
        rawb = [Buf() for _ in range(2)]
        sq = S.sb([128, 512], F32)
        bsq = Buf()
        r = S.sb([128, 512], F32)
        br = Buf()
        qn = S.sb([128, 512], F32)
        bqn = Buf()
        cs = [S.sb([128, 2, 512], F32) for _ in range(2)]
        csb = [Buf() for _ in range(2)]
        ta = S.sb([128, 512], F32)
        bta = Buf()
        for ci, (c0, n) in enumerate(token_chunks()):
            j = ci % 2
            S.dma(raw[j][:, :n], t.uT[row0:row0 + 128, c0:c0 + n], writes=[rawb[j]])
            S.op("act", lambda e, j=j, n=n: e.activation(out=sq[:, :n], in_=raw[j][:, :n], func=AF.Square), reads=[rawb[j]], writes=[bsq])
            S.op("pe", lambda e, n=n: e.matmul(self.ps[6][:, :n], lhsT=self.ones_f[:], rhs=sq[:, :n], start=True, stop=True), reads=[bsq, self.bconst], writes=[self.psb[6]])
            S.op("dve", lambda e, n=n: e.tensor_scalar(out=r[:, :n], in0=self.ps[6][:, :n], scalar1=1.0 / 128, scalar2=EPS, op0=ALU.mult, op1=ALU.add), reads=[self.psb[6]], writes=[br])
            S.op("act", lambda e, n=n: e.activation(out=r[:, :n], in_=r[:, :n], func=AF.Sqrt), reads=[br], writes=[br])
            S.op("dve", lambda e, n=n: e.reciprocal(out=r[:, :n], in_=r[:, :n]), reads=[br], writes=[br])
            S.op("dve", lambda e, j=j, n=n: e.scalar_tensor_tensor(out=qn[:, :n], in0=raw[j][:, :n], scalar=gain, in1=r[:, :n], op0=ALU.mult, op1=ALU.mult),
                 reads=[rawb[j], br, self.bconst], writes=[bqn])
            if rope and c0 < L:
                S.dma(cs[j][:, 0, :n], t.ropeC[:, c0:c0 + n], writes=[csb[j]])
                S.dma(cs[j][:, 1, :n], t.ropeS[:, c0:c0 + n], writes=[csb[j]])
                S.op("pe", lambda e, n=n: e.matmul(self.ps[7][:, :n], lhsT=self.perm[:], rhs=qn[:, :n], start=True, stop=True), reads=[bqn, self.bconst], writes=[self.psb[7]])
                S.op("dve", lambda e, j=j, n=n: e.tensor_tensor(out=ta[:, :n], in0=qn[:, :n], in1=cs[j][:, 0, :n], op=ALU.mult), reads=[bqn, csb[j]], writes=[bta])
                S.op("dve", lambda e, j=j, n=n: e.tensor_tensor(out=qn[:, :n], in0=self.ps[7][:, :n], in1=cs[j][:, 1, :n], op=ALU.mult), reads=[self.psb[7], csb[j]], writes=[bqn])
                S.op("dve", lambda e, n=n, c0=c0: e.tensor_tensor(out=dst[:, c0:c0 + n], in0=ta[:, :n], in1=qn[:, :n], op=ALU.add), reads=[bta, bqn], writes=[dstb])
            else:
                S.op("dve", lambda e, n=n, c0=c0: e.tensor_copy(out=dst[:, c0:c0 + n], in_=qn[:, :n]), reads=[bqn], writes=[dstb])
        S.barrier()
        S.release(m)

    def attn_alloc(self):
        S = self.S
        A = T()
        A.pt = [S.sb([128, 512], BF16) for _ in range(3)]
        A.ptb = [Buf() for _ in range(3)]
        A.acc = S.sb([128, 512], F32)
        A.accb = Buf()
        A.tb = [S.sb([128, 512], F32) for _ in range(3)]
        A.tbb = [Buf() for _ in range(3)]
        A.sb_ = [S.sb([128, 512], F32) for _ in range(2)]
        A.sbb = [Buf() for _ in range(2)]
        A.rinv = S.sb([128, 512], F32)
        A.rinvb = Buf()
        A.o = S.sb([128, 512], F32)
        A.ob_ = Buf()
        A.g = [S.sb([128, 512], F32) for _ in range(2)]
        A.gb = [Buf() for _ in range(2)]
        A.outb = [S.sb([128, 512], BF16) for _ in range(2)]
        A.outbb = [Buf() for _ in range(2)]
        A.k = 0
        A.call = 0
        return A

    def attention(self, A, QT, QTb, KT, KTb, Vt, Vtb, c0, n, keytiles, gate_row0, out_row0):
        S, t = self.S, self.t
        call = A.call
        A.call += 1
        po = 2 + call % 2
        pr = 4 + call % 2
        nk = len(keytiles)
        for idx, (kt, bt) in enumerate(keytiles):
            k = A.k
            A.k += 1
            pj = k % 2
            p3 = k % 3
            S.op("pe", lambda e, pj=pj, kt=kt: e.matmul(self.ps[pj][:, :n], lhsT=KT[:, kt * 128:(kt + 1) * 128], rhs=QT[:, c0:c0 + n], start=True, stop=True),
                 reads=[KTb, QTb], writes=[self.psb[pj]])
            if bt is not None:
                S.dma(A.tb[p3][:, :n], bt, writes=[A.tbb[p3]])
                S.op("dve", lambda e, pj=pj, p3=p3: e.scalar_tensor_tensor(out=A.sb_[pj][:, :n], in0=self.ps[pj][:, :n], scalar=SCALE, in1=A.tb[p3][:, :n], op0=ALU.mult, op1=ALU.add),
                     reads=[self.psb[pj], A.tbb[p3]], writes=[A.sbb[pj]])
                S.op("act", lambda e, pj=pj, p3=p3: e.activation(out=A.pt[p3][:, :n], in_=A.sb_[pj][:, :n], func=AF.Exp), reads=[A.sbb[pj]], writes=[A.ptb[p3]])
            else:
                S.op("act", lambda e, pj=pj, p3=p3: e.activation(out=A.pt[p3][:, :n], in_=self.ps[pj][:, :n], func=AF.Exp, scale=SCALE), reads=[self.psb[pj]], writes=[A.ptb[p3]])
            if idx == 0:
                S.op("pool", lambda e, p3=p3: e.tensor_copy(out=A.acc[:, :n], in_=A.pt[p3][:, :n]), reads=[A.ptb[p3]], writes=[A.accb])
            else:
                S.op("pool", lambda e, p3=p3: e.tensor_tensor(out=A.acc[:, :n], in0=A.acc[:, :n], in1=A.pt[p3][:, :n], op=ALU.add), reads=[A.ptb[p3]], writes=[A.accb])
            S.op("pe", lambda e, p3=p3, kt=kt, idx=idx: e.matmul(self.ps[po][:, :n], lhsT=Vt[:, kt, :], rhs=A.pt[p3][:, :n], start=(idx == 0), stop=(idx == nk - 1)),
                 reads=[Vtb, A.ptb[p3]], writes=[self.psb[po]])
        S.op("pe", lambda e: e.matmul(self.ps[pr][:, :n], lhsT=self.ones_f[:], rhs=A.acc[:, :n], start=True, stop=True), reads=[A.accb, self.bconst], writes=[self.psb[pr]])
        S.op("dve", lambda e: e.reciprocal(out=A.rinv[:, :n], in_=self.ps[pr][:, :n]), reads=[self.psb[pr]], writes=[A.rinvb])
        S.op("dve", lambda e: e.tensor_tensor(out=A.o[:, :n], in0=self.ps[po][:, :n], in1=A.rinv[:, :n], op=ALU.mult), reads=[self.psb[po], A.rinvb], writes=[A.ob_])
        j = call % 2
        S.dma(A.g[j][:, :n], t.uT[gate_row0:gate_row0 + 128, c0:c0 + n], writes=[A.gb[j]])
        S.op("act", lambda e: e.activation(out=A.g[j][:, :n], in_=A.g[j][:, :n], func=AF.Silu), reads=[A.gb[j]], writes=[A.gb[j]])
        S.op("dve", lambda e: e.tensor_tensor(out=A.outb[j][:, :n], in0=A.o[:, :n], in1=A.g[j][:, :n], op=ALU.mult), reads=[A.ob_, A.gb[j]], writes=[A.outbb[j]])
        S.dma(t.gTl[out_row0:out_row0 + 128, c0:c0 + n], A.outb[j][:, :n], reads=[A.outbb[j]])

    def stage_gqa(self, layer, need_ctx):
        S, t = self.S, self.t
        m = S.mark()
        KT = S.sb([128, NT], BF16, "KT")
        QT = S.sb([128, NT], BF16, "QT")
        Vt = S.sb([128, 66, 128], BF16, "Vt")
        KTb, QTb, Vtb = Buf(), Buf(), Buf()
        for q6 in range(6):
            S.dma(Vt[:, q6 * 11:(q6 + 1) * 11, :], t.uV[:, 0:128].rearrange("(kt p) d -> p kt d", p=128)[:, q6 * 11:(q6 + 1) * 11, :], writes=[Vtb])
        self.qk_prep(E_K, self.qkn[:, layer, 1:2], KT, KTb, True)
        A = self.attn_alloc()
        allk = [(kt, None) for kt in range(66)]
        for hq in range(2):
            self.qk_prep(E_Q + hq * 128, self.qkn[:, layer, 0:1], QT, QTb, True)
            for qc in range(16):
                self.attention(A, QT, QTb, KT, KTb, Vt, Vtb, qc * 512, 512, allk, E_AG + hq * 128, 256 + hq * 128)
            if need_ctx:
                self.attention(A, QT, QTb, KT, KTb, Vt, Vtb, L, LC, [(64, None), (65, None)], E_AG + hq * 128, 256 + hq * 128)
        S.barrier()
        S.release(m)

    def stage_na(self, layer, need_ctx):
        S, t = self.S, self.t
        i = layer // 2
        m = S.mark()
        KT = S.sb([128, NT], BF16, "KT")
        QT = S.sb([128, NT], BF16, "QT")
        Vt = S.sb([128, 66, 128], BF16, "Vt")
        KTb, QTb, Vtb = Buf(), Buf(), Buf()
        A = self.attn_alloc()
        for h in range(4):
            for q6 in range(6):
                S.dma(Vt[:, q6 * 11:(q6 + 1) * 11, :], t.uV[:, h * 128:(h + 1) * 128].rearrange("(kt p) d -> p kt d", p=128)[:, q6 * 11:(q6 + 1) * 11, :], writes=[Vtb])
            self.qk_prep(O_K + h * 128, self.qkn[:, layer, 1:2], KT, KTb, False)
            self.qk_prep(O_Q + h * 128, self.qkn[:, layer, 0:1], QT, QTb, False)
            for qb in range(16):
                kb = min(max(8 * qb - 4, 0), 112)
                var = 0 if qb == 0 else (2 if qb == 15 else 1)
                kts = [(kb // 2 + mm, t.btab[i][h, var, mm, :, :]) for mm in range(8)] + [(64, None), (65, None)]
                self.attention(A, QT, QTb, KT, KTb, Vt, Vtb, qb * 512, 512, kts, O_G + h * 128, h * 128)
            if need_ctx:
                self.attention(A, QT, QTb, KT, KTb, Vt, Vtb, L, LC, [(64, None), (65, None)], O_G + h * 128, h * 128)
        S.barrier()
        S.release(m)

    def dump(self, name, src, shape, dt=F32):
        dst = self.nc.dram_tensor(name, list(shape), dt, kind="ExternalOutput").ap()
        self.S.barrier()
        self.S.dma(dst, src)
        self.S.barrier()

    def build(self, dbg=None, stop=None):
        S, t = self.S, self.t

        def after(name, layer):
            if dbg:
                dbg(self, name, layer)
            return stop == (name, layer)

        self.consts()
        self.stage_ada()
        done = after("ada", 0)
        src = t.xT
        for layer in range(self.depth):
            if done:
                break
            need_ctx = layer < 3
            self.stage_norm(layer, src)
            if after("norm", layer):
                break
            self.stage_proj(layer)
            if after("proj", layer):
                break
            if layer % 2 == 0:
                self.stage_hyena(layer, need_ctx)
                if after("hyena", layer):
                    break
                self.stage_gqa(layer, need_ctx)
                if after("gqa", layer):
                    break
            else:
                self.stage_na(layer, need_ctx)
                if after("na", layer):
                    break
            self.allgather(t.gTl[:, :], t.gT[:, :])
            self.stage_out(layer, src, t.xs, last=(layer == self.depth - 1))
            if after("out", layer):
                break
            src = t.xs
        S.barrier()
        S.emit()


def _const_tables():
    f32 = np.float32
    out = {}
    zs, tl = [], []
    for Ls in (L, LC):
        tt = np.linspace(0.0, 1.0, Ls, dtype=f32)[:, None]
        w = (f32(2.0 * np.pi / Ls) * np.arange(Ls, dtype=f32))[:, None]
        bands = np.linspace(1e-4, 15, 16, dtype=f32)[None, :]
        z = np.concatenate([tt, np.cos(bands * w), -np.sin(bands * w)], axis=-1).astype(f32)
        for rev in (False, True):
            zz = z[::-1] if rev else z
            buf = np.zeros((33, L), f32)
            buf[:, :Ls] = zz.T
            zs.append(buf)
            tb = np.zeros((L,), f32)
            tb[:Ls] = tt[::-1, 0] if rev else tt[:, 0]
            tl.append(tb)
    out["zT"] = np.stack(zs)
    out["tlin"] = np.stack(tl)
    tpos = np.arange(L)
    rows = (tpos // 64).astype(f32)
    cols = (tpos % 64).astype(f32)
    inv = (f32(10000.0) ** (-np.arange(32, dtype=f32) / f32(32))).astype(f32)
    C = np.zeros((128, L), f32)
    Sg = np.zeros((128, L), f32)
    perm = np.zeros((128, 128), f32)
    for d in range(128):
        pos = rows if d < 64 else cols
        ang = (pos * inv[d % 32]).astype(f32)
        C[d] = np.cos(ang)
        first = (d % 64) < 32
        Sg[d] = -np.sin(ang) if first else np.sin(ang)
        partner = d + 32 if first else d - 32
        perm[partner, d] = 1.0
    out["ropeC"], out["ropeS"], out["perm"] = C, Sg, perm
    k = np.arange(128, dtype=np.float64)
    ang128 = 2.0 * np.pi * np.outer(k, k) / 128.0
    angN = 2.0 * np.pi * np.outer(k, k) / 16384.0
    dft = np.zeros((128, 1536), np.float64)
    dft[:64, 0:128] = np.cos(ang128[:64]); dft[:64, 128:256] = -np.sin(ang128[:64])
    dft[:, 256:384] = np.cos(angN); dft[:, 384:512] = -np.sin(angN)
    dft[:, 512:640] = np.cos(ang128); dft[:, 640:768] = -np.sin(ang128); dft[:, 768:896] = np.sin(ang128)
    dft[:, 896:1024] = np.cos(ang128); dft[:, 1024:1152] = np.sin(ang128)
    dft[:, 1152:1280] = -np.sin(ang128); dft[:, 1280:1408] = np.cos(ang128)
    dft[:, 1408:1472] = np.cos(ang128[:, :64]) / 16384.0; dft[:, 1472:1536] = -np.sin(ang128[:, :64]) / 16384.0
    out["dft"] = dft.astype(f32)
    return out


def _btab(rel_bias_heads):
    nh = rel_bias_heads.shape[0]
    tab = np.empty((nh, 3, 8, 128, 512), np.float32)
    p = np.arange(128)[:, None]
    f = np.arange(512)[None, :]
    for vi, qb in enumerate((0, 1, 15)):
        r0 = 8 * qb
        kb = min(max(r0 - 4, 0), 112)
        for m in range(8):
            kr = kb + 2 * m + p // 64
            kc = p % 64
            qr = r0 + f // 64
            qc = f % 64
            rs = np.clip(qr - 4, 0, 120)
            cs = np.clip(qc - 8, 0, 48)
            valid = (kr >= rs) & (kr < rs + 8) & (kc >= cs) & (kc < cs + 16)
            di = np.clip(kr - qr + 7, 0, 14)
            dj = np.clip(kc - qc + 15, 0, 30)
            g = rel_bias_heads[:, di, dj]
            tab[:, vi, m] = np.where(valid[None], g, np.float32(-30000.0))
    return tab


def _prep_inputs(inp):
    f32 = np.float32
    A = lambda k: np.asarray(inp[k], dtype=f32)
    x, c, ctx, c_ctx = A("x"), A("c"), A("ctx"), A("c_ctx")
    norm_g, ada_w, ada_b = A("norm_g"), A("ada_w"), A("ada_b")
    e_w_in, e_w_out, o_w_in, o_w_out = A("e_w_in"), A("e_w_out"), A("o_w_in"), A("o_w_out")
    consts = _const_tables()
    cvec = np.stack([c[0], c_ctx])
    cvT = np.ascontiguousarray(cvec.reshape(2, 32, 128).transpose(2, 0, 1))
    xfull = np.concatenate([x[0], ctx[0]], axis=0)
    maps = []
    for r in range(NCORES):
        d = dict(consts)
        fs = slice(FO * r, FO * r + FO)
        d["xT"] = np.ascontiguousarray(xfull[:, fs].T)
        d["cvT"] = cvT
        d["normg"] = np.ascontiguousarray(norm_g[:, fs].reshape(4, 4, 128).transpose(2, 0, 1))
        aw = ada_w.reshape(4, D, 3, D)[:, :, :, fs]
        d["adaw"] = np.ascontiguousarray(aw.reshape(4 * D, 3 * FO))
        d["adab"] = np.ascontiguousarray(ada_b.reshape(4, 3, D)[:, :, fs].reshape(4, 3, 4, 128).transpose(3, 0, 1, 2))
        qkn = np.zeros((128, 4, 2), f32)
        for l in range(4):
            i = l // 2
            if l % 2 == 0:
                ch = slice(256 * r, 256 * r + 256)
                kv = r // 2
                cols = np.concatenate([
                    np.arange(2048)[ch], 2048 + np.arange(2048)[ch], 4096 + np.arange(2048)[ch], 6144 + np.arange(2048)[ch],
                    8192 + 256 * r + np.arange(256), 10240 + 128 * kv + np.arange(128), 11264 + 256 * r + np.arange(256),
                    10752 + 128 * kv + np.arange(128)])
                d[f"win{l}"] = np.ascontiguousarray(e_w_in[i][:, cols])
                rowperm = np.concatenate([np.concatenate([256 * rr + np.arange(256), 2048 + 256 * rr + np.arange(256)]) for rr in range(8)])
                d[f"wout{l}"] = np.ascontiguousarray(e_w_out[i][rowperm][:, fs])
                qkn[:, l, 0] = A("gqa_q_norm")[i]
                qkn[:, l, 1] = A("gqa_k_norm")[i]
            else:
                cols = np.concatenate([512 * r + np.arange(512), 4096 + 512 * r + np.arange(512), 3 * 4096 + 512 * r + np.arange(512),
                                       2 * 4096 + 512 * r + np.arange(512)])
                d[f"win{l}"] = np.ascontiguousarray(o_w_in[i][:, cols])
                d[f"wout{l}"] = np.ascontiguousarray(o_w_out[i][:, fs])
                qkn[:, l, 0] = A("na_q_norm")[i]
                qkn[:, l, 1] = A("na_k_norm")[i]
        d["qkn"] = qkn
        for i in range(2):
            ch = 256 * r
            hc = A("hy_conv")[i].reshape(3, 3, 2048)[:, :, ch:ch + 256]
            d[f"hyconv{i}"] = np.ascontiguousarray(hc.reshape(3, 3, 2, 128).transpose(3, 2, 1, 0))
            d[f"hyw1_{i}"] = A("hy_w1")[i]
            d[f"hyw2_{i}"] = A("hy_w2")[i]
            d[f"hyvec{i}"] = np.ascontiguousarray(np.stack([A("hy_b1")[i], A("hy_b2")[i], A("hy_freq")[i]], axis=1))
            w3 = A("hy_w3")[i].reshape(64, 2, 2, 2048)[:, :, :, ch:ch + 256]
            d[f"hyw3_{i}"] = np.ascontiguousarray(w3.reshape(64, 1024))
            dec = A("hy_decay")[i][:, ch:ch + 256].reshape(2, 2, 128)
            dsk = A("hy_dskip")[i][:, ch:ch + 256].reshape(2, 2, 128)
            d[f"hydd{i}"] = np.ascontiguousarray(np.stack([dec, dsk], axis=0).transpose(3, 2, 0, 1))
            d[f"btab{i}"] = _btab(A("na_rel_bias")[i][4 * r:4 * r + 4])
        maps.append(d)
    return maps


_CACHE = {}


def kernel(**inputs):
    maps = _prep_inputs(inputs)
    if "nc" not in _CACHE:
        nc = bass.Bass("TRN2", target_bir_lowering=False)
        kb = K(nc)
        kb.build()
        _CACHE["nc"] = nc
    nc = _CACHE["nc"]
    res = run_bass_kernel_spmd(nc, maps, core_ids=list(range(NCORES)))
    out = np.empty((1, L, D), np.float32)
    for r in range(NCORES):
        out[0][:, FO * r:FO * r + FO] = np.asarray(res.results[r]["outT"]).T
    return out
```

```python
import contextlib
import numpy as np
import concourse.bass as bass
import concourse.mybir as mybir
from concourse.bass_utils import run_bass_kernel_spmd

F32 = mybir.dt.float32
BF16 = mybir.dt.bfloat16
AF = mybir.ActivationFunctionType
ALU = mybir.AluOpType
AX = mybir.AxisListType

NCORES = 8


class Buf:
    __slots__ = ("name", "w", "r")

    def __init__(self, name=""):
        self.name = name
        self.w = None
        self.r = {}


class Sched:
    ENGS = ("pe", "act", "dve", "pool", "sp")
    CE = ("pe", "act", "dve", "pool")

    def __init__(self, nc, dma_slots=None):
        self.nc = nc
        self.q = {e: [] for e in self.ENGS}
        self.cnt = {e: 0 for e in self.CE}
        self.seen = {e: {} for e in self.ENGS}
        self.dq = {"sp": [0, 24], "pool": [0, 8], "act": [0, 8]}
        if dma_slots:
            for k, v in dma_slots.items():
                self.dq[k][1] = v
        self.semkeys = set()
        self.sb_off = 16512
        self.sb_max = 0
        self.nalloc = 0

    def sb(self, shape, dtype, name=None):
        per = int(np.prod(shape[1:])) * (2 if dtype == BF16 else 4)
        off = (self.sb_off + 63) // 64 * 64
        self.nalloc += 1
        t = self.nc.alloc_sbuf_tensor_at(f"{name or 't'}_{self.nalloc}", list(shape), dtype, offset=off)
        self.sb_off = off + per
        self.sb_max = max(self.sb_max, self.sb_off)
        assert self.sb_off <= 229000, f"SBUF overflow {self.sb_off}"
        return t

    def mark(self):
        return self.sb_off

    def release(self, m):
        self.sb_off = m

    def _deps(self, reads, writes):
        deps = {}

        def add(tok):
            if tok is None:
                return
            k, v = tok
            if deps.get(k, 0) < v:
                deps[k] = v

        for b in reads:
            add(b.w)
        for b in writes:
            add(b.w)
            for k, v in b.r.items():
                add((k, v))
        return deps

    def _emit_waits(self, eng, deps, skip_own=False):
        own = "c:" + eng
        for k, v in deps.items():
            if k == own and (eng == "pe" or skip_own):
                continue
            if self.seen[eng].get(k, 0) >= v:
                continue
            self.seen[eng][k] = v
            self.q[eng].append(("w", k, v))

    def _commit(self, tok, reads, writes):
        k, v = tok
        for b in reads:
            if b.r.get(k, 0) < v:
                b.r[k] = v
        for b in writes:
            b.w = tok
            b.r = {}

    def op(self, eng, fn, reads=(), writes=(), skip_own=False):
        deps = self._deps(reads, writes)
        self._emit_waits(eng, deps, skip_own)
        self.cnt[eng] += 1
        own = "c:" + eng
        self.semkeys.add(own)
        tok = (own, self.cnt[eng])
        self.q[eng].append(("o", fn, own, 1))
        self._commit(tok, reads, writes)
        return tok

    def dma(self, out, in_, reads=(), writes=(), q="sp", **kw):
        st = self.dq[q]
        i = st[0]
        st[0] += 1
        R = st[1]
        slot = i % R
        key = f"d:{q}:{slot}"
        self.semkeys.add(key)
        deps = self._deps(reads, writes)
        if i >= R:
            v0 = 16 * (i // R)
            if deps.get(key, 0) < v0:
                deps[key] = v0
        self._emit_waits(q, deps)
        tok = (key, 16 * (i // R + 1))
        self.q[q].append(("o", (lambda e, o=out, s=in_, kw=kw: e.dma_start(out=o, in_=s, **kw)), key, 16))
        self._commit(tok, reads, writes)
        return tok

    def custom(self, eng, fn, key, amount, value, reads=(), writes=()):
        deps = self._deps(reads, writes)
        self._emit_waits(eng, deps)
        self.semkeys.add(key)
        self.q[eng].append(("o", fn, key, amount))
        tok = (key, value)
        self._commit(tok, reads, writes)
        return tok

    def barrier(self):
        allv = {}
        for e in self.CE:
            if self.cnt[e]:
                allv["c:" + e] = self.cnt[e]
        for q, (n, R) in self.dq.items():
            for slot in range(min(n, R)):
                uses = (n - 1 - slot) // R + 1
                allv[f"d:{q}:{slot}"] = 16 * uses
        for k, v in getattr(self, "extra_sems", {}).items():
            allv[k] = v
        for e in self.ENGS:
            self._emit_waits(e, dict(allv), skip_own=True)

    def emit(self):
        nc = self.nc
        with contextlib.ExitStack() as es:
            sems = {}
            for k in sorted(self.semkeys):
                sems[k] = es.enter_context(nc.semaphore(k.replace(":", "_")))
            block = es.enter_context(nc.Block())

            def run(engname):
                def body(e):
                    for it in self.q[engname]:
                        if it[0] == "w":
                            e.wait_ge(sems[it[1]], it[2])
                        else:
                            ins = it[1](e)
                            ins.then_inc(sems[it[2]], it[3])
                return body

            block.tensor(run("pe"))
            block.scalar(run("act"))
            block.vector(run("dve"))
            block.gpsimd(run("pool"))
            block.sync(run("sp"))


L = 8192
LC = 256
NT = L + LC
D = 4096
FO = 512
EPS = 1e-6
SCALE = 128 ** -0.5
PI = float(np.pi)
E_V, E_X1, E_X2, E_HG, E_Q, E_K, E_AG = 0, 256, 512, 768, 1024, 1280, 1408
E_FM = 1664
E_NV = 128
O_Q, O_K, O_G = 0, 512, 1024
O_FM = 1536
O_NV = 512


def run_rr(factories, width):
    free = list(range(width))
    active = []
    it = iter(factories)
    done = False
    while True:
        while free and not done:
            f = next(it, None)
            if f is None:
                done = True
                break
            slot = free.pop(0)
            active.append((slot, f(slot)))
        if not active:
            break
        for item in list(active):
            slot, g = item
            try:
                next(g)
            except StopIteration:
                active.remove(item)
                free.append(slot)


class T:
    pass


def token_chunks():
    ch = [(c * 512, 512) for c in range(16)]
    ch.append((L, LC))
    return ch


class K:
    def __init__(self, nc, depth=4, dbg=None):
        self.nc = nc
        self.S = Sched(nc)
        self.depth = depth
        self.dbg = dbg or {}
        self.ncc = 0
        self.use_fft = True
        self.S.extra_sems = {}
        self.P2 = [nc.alloc_psum_tensor(f"psq{i}", [128, 1024], F32) for i in range(4)]
        self.ps = [self.P2[i // 2][:, (i % 2) * 512:(i % 2) * 512 + 512] for i in range(8)]
        self.psb = [Buf() for _ in range(8)]
        self.t = T()
        self.alloc_dram()

    def din(self, name, shape, dt=F32):
        return self.nc.dram_tensor(name, list(shape), dt, kind="ExternalInput").ap()

    def dscr(self, name, shape, dt=F32):
        return self.nc.dram_tensor(name, list(shape), dt).ap()

    def alloc_dram(self):
        t = self.t
        t.xT = self.din("xT", [FO, NT])
        t.cvT = self.din("cvT", [128, 2, 32])
        t.normg = self.din("normg", [128, 4, 4])
        t.adaw = self.din("adaw", [4 * D, 3 * FO])
        t.adab = self.din("adab", [128, 4, 3, 4])
        t.win = [self.din(f"win{l}", [D, (E_FM + E_NV) if l % 2 == 0 else (O_FM + O_NV)]) for l in range(4)]
        t.wout = [self.din(f"wout{l}", [D, FO]) for l in range(4)]
        t.hyconv = [self.din(f"hyconv{i}", [128, 2, 3, 3]) for i in range(2)]
        t.hyw1 = [self.din(f"hyw1_{i}", [33, 64]) for i in range(2)]
        t.hyw2 = [self.din(f"hyw2_{i}", [64, 64]) for i in range(2)]
        t.hyvec = [self.din(f"hyvec{i}", [64, 3]) for i in range(2)]
        t.hyw3 = [self.din(f"hyw3_{i}", [64, 1024]) for i in range(2)]
        t.hydd = [self.din(f"hydd{i}", [128, 2, 2, 2]) for i in range(2)]
        t.qkn = self.din("qkn", [128, 4, 2])
        t.btab = [self.din(f"btab{i}", [4, 3, 8, 128, 512]) for i in range(2)]
        t.zT = self.din("zT", [4, 33, L])
        t.tlin = self.din("tlin", [4, L])
        t.ropeC = self.din("ropeC", [128, L])
        t.ropeS = self.din("ropeS", [128, L])
        t.perm = self.din("perm", [128, 128])
        t.dft = self.din("dft", [128, 1536])
        t.out = self.nc.dram_tensor("outT", [FO, L], F32, kind="ExternalOutput").ap()
        t.xs = self.dscr("xs", [FO, NT])
        t.ssl = self.dscr("ssl", [1, NT])
        t.ssa = self.dscr("ssa", [8, NT])
        t.hTl = self.dscr("hTl", [FO, NT], BF16)
        t.hT = self.dscr("hT", [D, NT], BF16)
        t.uT = self.dscr("uT", [E_FM, NT])
        t.uV = self.dscr("uV", [NT, 512], BF16)
        t.hd2 = self.dscr("hd2", [4, 64, L])
        t.zsc = self.dscr("zsc", [128, L], BF16)
        t.fsc = self.dscr("fsc", [2, 128, L], BF16)
        t.ysc = self.dscr("ysc", [128, L])
        t.gTl = self.dscr("gTl", [FO, NT], BF16)
        t.gT = self.dscr("gT", [D, NT], BF16)

    def allgather(self, src, dst):
        S = self.S
        S.barrier()
        self.ncc += 1
        n = self.ncc
        S.custom("pool", lambda e: e.collective_compute("AllGather", ALU.bypass, replica_groups=[list(range(NCORES))],
                                                         ins=[src], outs=[dst]), "cc", 1, n)
        S.extra_sems["cc"] = n
        S.barrier()

    def consts(self):
        S = self.S
        self.ones_f = S.sb([128, 128], F32, "ones")
        self.ones_b = S.sb([128, 128], BF16, "onesb")
        self.bconst = Buf()
        S.op("pool", lambda e: e.memset(self.ones_f[:], 1.0), writes=[self.bconst])
        S.op("pool", lambda e: e.memset(self.ones_b[:], 1.0), writes=[self.bconst])
        self.perm = S.sb([128, 128], F32, "perm")
        S.dma(self.perm[:], self.t.perm[:, :], writes=[self.bconst])
        self.mods = S.sb([128, 4, 3, 4, 2], F32, "mods")
        self.bmods = Buf()
        self.AB = S.sb([128, 4, 2, 4, 2], F32, "AB")
        self.qkn = S.sb([128, 4, 2], F32, "qkn")
        S.dma(self.qkn[:], self.t.qkn[:, :, :], writes=[self.bconst])
        dftf = S.sb([128, 1536], F32, "dftf")
        S.dma(dftf[:], self.t.dft[:, :], writes=[self.bconst])
        self.TWre = dftf[:, 256:384]
        self.TWim = dftf[:, 384:512]
        self.dftb = S.sb([128, 1536], BF16, "dftb")
        S.op("dve", lambda e: e.tensor_copy(out=self.dftb[:], in_=dftf[:]), reads=[self.bconst], writes=[self.bconst])
        db = self.dftb
        self.F1cat = db[0:64, 0:256]
        self.F2re, self.F2im, self.F2imn = db[:, 512:640], db[:, 640:768], db[:, 768:896]
        self.R1, self.R2 = db[:, 896:1152], db[:, 1152:1408]
        self.G1re, self.G1im = db[:, 1408:1472], db[:, 1472:1536]
        self.base_mark = S.mark()

    def stage_ada(self):
        S, t = self.S, self.t
        m = S.mark()
        cv = S.sb([128, 2, 32], F32)
        cs = S.sb([128, 32, 2], F32)
        bcv = Buf()
        S.dma(cv[:], t.cvT[:, :, :], writes=[bcv])
        S.op("act", lambda e: e.activation(out=cs[:].rearrange("p k v -> p v k"), in_=cv[:], func=AF.Silu), reads=[bcv], writes=[bcv])
        ab = S.sb([128, 4, 3, 4], F32)
        ng = S.sb([128, 4, 4], F32)
        bab = Buf()
        S.dma(ab[:], t.adab[:, :, :, :], writes=[bab])
        S.dma(ng[:], t.normg[:, :, :], writes=[bab])
        wb = [S.sb([128, 3 * FO], F32) for _ in range(3)]
        wbb = [Buf() for _ in range(3)]
        nw = 0
        for layer in range(self.depth):
            for half in range(2):
                for kc in range(32):
                    i = nw % 3
                    nw += 1
                    r0 = layer * D + kc * 128
                    S.dma(wb[i][:], t.adaw[r0:r0 + 128, :], writes=[wbb[i]])
                    for ch in range(3):
                        for f2 in range(2):
                            fc = half * 2 + f2
                            g = ch * 2 + f2
                            S.op("pe", lambda e, g=g, w=wb[i][:, ch * FO + fc * 128: ch * FO + fc * 128 + 128], r=cs[:, kc, :], kc=kc:
                                 e.matmul(self.ps[g][:, 0:2], lhsT=w, rhs=r, start=(kc == 0), stop=(kc == 31)),
                                 reads=[wbb[i], bcv], writes=[self.psb[g]])
                for ch in range(3):
                    for f2 in range(2):
                        fc = half * 2 + f2
                        g = ch * 2 + f2
                        S.op("dve", lambda e, g=g, d=self.mods[:, layer, ch, fc, :], b=ab[:, layer, ch, fc:fc + 1]:
                             e.tensor_scalar_add(out=d, in0=self.ps[g][:, 0:2], scalar1=b), reads=[self.psb[g], bab], writes=[self.bmods])
            for v in range(2):
                S.op("dve", lambda e, layer=layer, v=v: e.scalar_tensor_tensor(
                    out=self.AB[:, layer, 0, :, v], in0=self.mods[:, layer, 1, :, v], scalar=1.0, in1=ng[:, layer, :],
                    op0=ALU.add, op1=ALU.mult), reads=[self.bmods, bab], writes=[self.bmods])
                S.op("dve", lambda e, layer=layer, v=v: e.tensor_copy(out=self.AB[:, layer, 1, :, v], in_=self.mods[:, layer, 0, :, v]),
                     reads=[self.bmods], writes=[self.bmods])
        S.barrier()
        S.release(m)

    def stage_norm(self, layer, src):
        S, t = self.S, self.t
        m = S.mark()
        chunks = token_chunks()
        xb = [S.sb([128, 4, 512], F32) for _ in range(2)]
        xbb = [Buf() for _ in range(2)]
        sq = S.sb([128, 4, 512], F32)
        bsq = Buf()
        row = [S.sb([1, 512], F32) for _ in range(2)]
        rowb = [Buf() for _ in range(2)]
        for ci, (c0, n) in enumerate(chunks):
            i = ci % 2
            S.dma(xb[i][:, :, :n], src.rearrange("(fc p) t -> p fc t", p=128)[:, :, c0:c0 + n], writes=[xbb[i]])
            S.op("act", lambda e, i=i, n=n: e.activation(out=sq[:, :, :n], in_=xb[i][:, :, :n], func=AF.Square), reads=[xbb[i]], writes=[bsq])
            pj = ci % 2
            for fc in range(4):
                S.op("pe", lambda e, fc=fc, n=n, pj=pj: e.matmul(self.ps[pj][:, :n], lhsT=self.ones_f[:], rhs=sq[:, fc, :n], start=(fc == 0), stop=(fc == 3)),
                     reads=[bsq, self.bconst], writes=[self.psb[pj]])
            S.op("dve", lambda e, i=i, n=n, pj=pj: e.tensor_copy(out=row[i][:, :n], in_=self.ps[pj][0:1, :n]), reads=[self.psb[pj]], writes=[rowb[i]])
            S.dma(t.ssl[:, c0:c0 + n], row[i][:, :n], reads=[rowb[i]])
        self.allgather(t.ssl[:, :], t.ssa[:, :])
        s8 = [S.sb([8, 512], F32) for _ in range(2)]
        s8b = [Buf() for _ in range(2)]
        rstd = S.sb([128, 512], F32)
        brs = Buf()
        hb = [S.sb([128, 4, 512], BF16) for _ in range(2)]
        hbb = [Buf() for _ in range(2)]
        tmp = S.sb([128, 512], F32)
        btmp = Buf()
        for ci, (c0, n) in enumerate(chunks):
            i = ci % 2
            v = 1 if c0 >= L else 0
            S.dma(xb[i][:, :, :n], src.rearrange("(fc p) t -> p fc t", p=128)[:, :, c0:c0 + n], writes=[xbb[i]])
            S.dma(s8[i][:, :n], t.ssa[:, c0:c0 + n], writes=[s8b[i]])
            pj = 2 + ci % 2
            S.op("pe", lambda e, i=i, n=n, pj=pj: e.matmul(self.ps[pj][:, :n], lhsT=self.ones_f[0:8, :], rhs=s8[i][:, :n], start=True, stop=True),
                 reads=[s8b[i], self.bconst], writes=[self.psb[pj]])
            S.op("dve", lambda e, n=n, pj=pj: e.tensor_scalar(out=rstd[:, :n], in0=self.ps[pj][:, :n], scalar1=1.0 / D, scalar2=EPS, op0=ALU.mult, op1=ALU.add),
                 reads=[self.psb[pj]], writes=[brs])
            S.op("act", lambda e, n=n: e.activation(out=rstd[:, :n], in_=rstd[:, :n], func=AF.Sqrt), reads=[brs], writes=[brs])
            S.op("dve", lambda e, n=n: e.reciprocal(out=rstd[:, :n], in_=rstd[:, :n]), reads=[brs], writes=[brs])
            for fc in range(4):
                S.op("dve", lambda e, i=i, n=n, fc=fc: e.tensor_tensor(out=tmp[:, :n], in0=xb[i][:, fc, :n], in1=rstd[:, :n], op=ALU.mult),
                     reads=[xbb[i], brs], writes=[btmp])
                S.op("dve", lambda e, i=i, n=n, fc=fc, v=v: e.tensor_scalar(out=hb[i][:, fc, :n], in0=tmp[:, :n],
                                                                          scalar1=self.AB[:, layer, 0, fc, v:v + 1], scalar2=self.AB[:, layer, 1, fc, v:v + 1],
                                                                          op0=ALU.mult, op1=ALU.add),
                     reads=[btmp, self.bmods], writes=[hbb[i]])
            S.dma(t.hTl.rearrange("(fc p) t -> p fc t", p=128)[:, :, c0:c0 + n], hb[i][:, :, :n], reads=[hbb[i]])
        self.allgather(t.hTl[:, :], t.hT[:, :])
        S.release(m)

    def gemm(self, W, ncols_fm, src, epilogue, ncols_tm=0, epilogue_tm=None, chunks=None):
        S = self.S
        m = S.mark()
        chunks = chunks or token_chunks()
        nblk = ncols_fm // 128
        GB = 8
        groups = [[list(range(g, min(g + GB, nblk))), 0] for g in range(0, nblk, GB)]
        if ncols_tm:
            if len(groups[-1][0]) * 128 + ncols_tm <= GB * 128:
                groups[-1][1] = ncols_tm
            else:
                groups.append([[], ncols_tm])
        wg = S.sb([128, 32, GB * 128], BF16, "wg")
        bwg = Buf()
        wst = [S.sb([128, GB * 128], F32, "wst") for _ in range(2)]
        wstb = [Buf() for _ in range(2)]
        hx = [S.sb([128, 32, 512], BF16, "hx") for _ in range(2)]
        hxb = [Buf() for _ in range(2)]
        nld = 0
        pk = 0
        for (blks, ntm) in groups:
            nfm = len(blks) * 128
            for kc in range(32):
                i = kc % 2
                if nfm:
                    S.dma(wst[i][:, :nfm], W[kc * 128:(kc + 1) * 128, blks[0] * 128:blks[0] * 128 + nfm], writes=[wstb[i]])
                if ntm:
                    S.dma(wst[i][:, nfm:nfm + ntm], W[kc * 128:(kc + 1) * 128, ncols_fm:ncols_fm + ntm], writes=[wstb[i]])
                eng = "pool" if kc % 2 == 0 else "act"
                if eng == "pool":
                    S.op("pool", lambda e, i=i, kc=kc, w=nfm + ntm: e.tensor_copy(out=wg[:, kc, :w], in_=wst[i][:, :w]), reads=[wstb[i]], writes=[bwg])
                else:
                    S.op("act", lambda e, i=i, kc=kc, w=nfm + ntm: e.activation(out=wg[:, kc, :w], in_=wst[i][:, :w], func=AF.Copy), reads=[wstb[i]], writes=[bwg])
            for (t0, n) in chunks:
                i = nld % 2
                nld += 1
                for q4 in range(4):
                    S.dma(hx[i][:, q4 * 8:(q4 + 1) * 8, :n], src.rearrange("(kc p) t -> p kc t", p=128)[:, q4 * 8:(q4 + 1) * 8, t0:t0 + n], writes=[hxb[i]])
                for bi, cb in enumerate(blks):
                    pj = 4 + pk % 4
                    pk += 1
                    for kc in range(32):
                        S.op("pe", lambda e, i=i, kc=kc, bi=bi, pj=pj, n=n: e.matmul(
                            self.ps[pj][:, :n], lhsT=wg[:, kc, bi * 128:(bi + 1) * 128], rhs=hx[i][:, kc, :n],
                            start=(kc == 0), stop=(kc == 31)), reads=[hxb[i], bwg], writes=[self.psb[pj]])
                    epilogue(self.ps[pj][:, :n], self.psb[pj], cb, t0, n)
                if ntm:
                    for sub in range(n // 128):
                        pj = 4 + pk % 4
                        pk += 1
                        for kc in range(32):
                            S.op("pe", lambda e, i=i, kc=kc, sub=sub, pj=pj, nfm=nfm, ntm=ntm: e.matmul(
                                self.ps[pj][:, :ntm], lhsT=hx[i][:, kc, sub * 128:(sub + 1) * 128], rhs=wg[:, kc, nfm:nfm + ntm],
                                start=(kc == 0), stop=(kc == 31)), reads=[hxb[i], bwg], writes=[self.psb[pj]])
                        epilogue_tm(self.ps[pj][:, :ntm], self.psb[pj], t0 + sub * 128)
        S.barrier()
        S.release(m)

    def stage_proj(self, layer):
        S, t = self.S, self.t
        even = layer % 2 == 0
        m = S.mark()
        st = [S.sb([128, 512], F32, "pst") for _ in range(4)]
        stb = [Buf() for _ in range(4)]
        sv = [S.sb([128, 512], BF16, "psv") for _ in range(2)]
        svb = [Buf() for _ in range(2)]
        cnt = [0, 0]

        def epi(ps, psb, cb, t0, n):
            i = cnt[0] % 4
            cnt[0] += 1
            eng = "act" if i % 2 == 0 else "dve"
            if eng == "act":
                S.op("act", lambda e, i=i, n=n: e.activation(out=st[i][:, :n], in_=ps, func=AF.Copy), reads=[psb], writes=[stb[i]])
            else:
                S.op("dve", lambda e, i=i, n=n: e.tensor_copy(out=st[i][:, :n], in_=ps), reads=[psb], writes=[stb[i]])
            S.dma(t.uT[cb * 128:(cb + 1) * 128, t0:t0 + n], st[i][:, :n], reads=[stb[i]])

        def epi_tm(ps, psb, tok0):
            i = cnt[1] % 2
            cnt[1] += 1
            ncw = E_NV if even else O_NV
            S.op("act", lambda e, i=i: e.activation(out=sv[i][:, :ncw], in_=ps, func=AF.Copy), reads=[psb], writes=[svb[i]])
            S.dma(t.uV[tok0:tok0 + 128, 0:ncw], sv[i][:, :ncw], reads=[svb[i]])

        self.gemm(t.win[layer], E_FM if even else O_FM, t.hT, epi, E_NV if even else O_NV, epi_tm)
        S.release(m)

    def stage_out(self, layer, src, dst, last):
        S, t = self.S, self.t
        m = S.mark()
        xin = [S.sb([128, 512], F32, "oxi") for _ in range(3)]
        xinb = [Buf() for _ in range(3)]
        cnt = [0]

        def epi(ps, psb, cb, t0, n):
            i = cnt[0] % 3
            cnt[0] += 1
            v = 1 if t0 >= L else 0
            S.dma(xin[i][:, :n], src[cb * 128:(cb + 1) * 128, t0:t0 + n], writes=[xinb[i]])
            S.op("dve", lambda e, i=i, n=n, cb=cb, v=v: e.scalar_tensor_tensor(
                out=xin[i][:, :n], in0=ps, scalar=self.mods[:, layer, 2, cb, v:v + 1], in1=xin[i][:, :n], op0=ALU.mult, op1=ALU.add),
                reads=[psb, xinb[i], self.bmods], writes=[xinb[i]])
            if last:
                S.dma(t.out[cb * 128:(cb + 1) * 128, t0:t0 + n], xin[i][:, :n], reads=[xinb[i]])
            else:
                S.dma(dst[cb * 128:(cb + 1) * 128, t0:t0 + n], xin[i][:, :n], reads=[xinb[i]])

        chunks = token_chunks()
        if last:
            chunks = chunks[:-1]
        self.gemm(t.wout[layer], FO, t.gT, epi, chunks=chunks)
        S.release(m)

    def sin_layer(self, ps, psb, freq, fbcol, out, outb, n, a, tm, ba):
        S = self.S
        S.op("dve", lambda e: e.tensor_scalar(out=a[:, :n], in0=ps, scalar1=freq, scalar2=fbcol, op0=ALU.mult, op1=ALU.add), reads=[psb, self.bhyw], writes=[ba])
        S.op("dve", lambda e: e.tensor_scalar(out=tm[:, :n], in0=a[:, :n], scalar1=PI, scalar2=-2 * PI, op0=ALU.is_gt, op1=ALU.mult), reads=[ba], writes=[ba])
        S.op("dve", lambda e: e.tensor_tensor(out=a[:, :n], in0=a[:, :n], in1=tm[:, :n], op=ALU.add), reads=[ba], writes=[ba])
        S.op("dve", lambda e: e.tensor_scalar(out=tm[:, :n], in0=a[:, :n], scalar1=-PI, scalar2=2 * PI, op0=ALU.is_lt, op1=ALU.mult), reads=[ba], writes=[ba])
        S.op("dve", lambda e: e.tensor_tensor(out=a[:, :n], in0=a[:, :n], in1=tm[:, :n], op=ALU.add), reads=[ba], writes=[ba])
        S.op("act", lambda e: e.activation(out=out, in_=a[:, :n], func=AF.Sin), reads=[ba], writes=[outb])

    def stage_hyena(self, layer, need_ctx):
        S, t = self.S, self.t
        i = layer // 2
        m = S.mark()
        w1 = S.sb([33, 64], F32)
        w2 = S.sb([64, 64], F32)
        vec = S.sb([64, 3], F32)
        fb = S.sb([64, 2], F32)
        w3 = S.sb([64, 1024], F32)
        cw = S.sb([128, 2, 3, 3], F32)
        dd = S.sb([128, 2, 2, 2], F32)
        nd = S.sb([128, 2, 2], F32)
        self.bhyw = bw = Buf()
        S.dma(w1[:], t.hyw1[i][:, :], writes=[bw])
        S.dma(w2[:], t.hyw2[i][:, :], writes=[bw])
        S.dma(vec[:], t.hyvec[i][:, :], writes=[bw])
        S.dma(w3[:], t.hyw3[i][:, :], writes=[bw])
        S.dma(cw[:], t.hyconv[i][:, :, :, :], writes=[bw])
        S.dma(dd[:], t.hydd[i][:, :, :, :], writes=[bw])
        S.op("dve", lambda e: e.tensor_tensor(out=fb[:, 0:1], in0=vec[:, 0:1], in1=vec[:, 2:3], op=ALU.mult), reads=[bw], writes=[bw])
        S.op("dve", lambda e: e.tensor_tensor(out=fb[:, 1:2], in0=vec[:, 1:2], in1=vec[:, 2:3], op=ALU.mult), reads=[bw], writes=[bw])
        S.op("act", lambda e: e.activation(out=nd[:], in_=dd[:, :, 0, :], func=AF.Abs), reads=[bw], writes=[bw])
        S.op("dve", lambda e: e.tensor_scalar_mul(out=nd[:], in0=nd[:], scalar1=-1.0), reads=[bw], writes=[bw])
        m2 = S.mark()
        zt = [S.sb([33, 512], F32) for _ in range(2)]
        ztb = [Buf() for _ in range(2)]
        h1 = S.sb([64, 512], F32)
        h2 = [S.sb([64, 512], F32) for _ in range(2)]
        h1b = Buf()
        h2b = [Buf() for _ in range(2)]
        a = S.sb([64, 512], F32)
        tm = S.sb([64, 512], F32)
        ba = Buf()
        k = 0
        for vi in range(4):
            Ls = L if vi < 2 else LC
            if vi >= 2 and not need_ctx:
                continue
            for c0 in range(0, Ls, 512):
                n = min(512, Ls - c0)
                j = k % 2
                k += 1
                S.dma(zt[j][:, :n], t.zT[vi, :, c0:c0 + n], writes=[ztb[j]])
                S.op("pe", lambda e, j=j, n=n: e.matmul(self.ps[0][0:64, :n], lhsT=w1[:], rhs=zt[j][:, :n], start=True, stop=True), reads=[bw, ztb[j]], writes=[self.psb[0]])
                self.sin_layer(self.ps[0][0:64, :n], self.psb[0], vec[:, 2:3], fb[:, 0:1], h1[:, :n], h1b, n, a, tm, ba)
                S.op("pe", lambda e, n=n: e.matmul(self.ps[1][0:64, :n], lhsT=w2[:], rhs=h1[:, :n], start=True, stop=True), reads=[bw, h1b], writes=[self.psb[1]])
                self.sin_layer(self.ps[1][0:64, :n], self.psb[1], vec[:, 2:3], fb[:, 1:2], h2[j][:, :n], h2b[j], n, a, tm, ba)
                S.dma(t.hd2[vi, :, c0:c0 + n], h2[j][:, :n], reads=[h2b[j]])
        S.barrier()
        S.release(m2)
        z = S.sb([128, L], F32, "hz")
        hk2 = S.sb([128, 2 * L if not self.use_fft else 512], F32, "hk2")
        acc = S.sb([128, L], F32, "hacc")
        bz, bhk, bacc = Buf(), Buf(), Buf()
        raw = [S.sb([128, 514], F32) for _ in range(2)]
        rawb = [Buf() for _ in range(2)]
        xc = S.sb([128, 512], F32)
        bxc = Buf()
        tmp = S.sb([128, 512], F32)
        btmp = Buf()
        ob = [S.sb([128, 512], BF16) for _ in range(2)]
        obb = [Buf() for _ in range(2)]
        hch = [S.sb([64, 512], F32) for _ in range(2)]
        hchb = [Buf() for _ in range(2)]
        tl = [S.sb([128, 512], F32) for _ in range(2)]
        tlb = [Buf() for _ in range(2)]
        win = S.sb([128, 512], F32)
        bwin = Buf()
        cnt = {"raw": 0, "h": 0, "ob": 0}

        def conv_chunk(blk, kind, row0, col0, Ls, c0, n, out, outb):
            j = cnt["raw"] % 2
            cnt["raw"] += 1
            lo, hi = c0 - 1, c0 + n + 1
            slo, shi = max(lo, 0), min(hi, Ls)
            S.op("pool", lambda e, j=j: e.memset(raw[j][:, :], 0.0), writes=[rawb[j]])
            S.dma(raw[j][:, slo - lo:shi - lo], t.uT[row0 + blk * 128:row0 + blk * 128 + 128, col0 + slo:col0 + shi], writes=[rawb[j]])
            S.op("dve", lambda e, j=j: e.tensor_scalar_mul(out=out, in0=raw[j][:, 1:n + 1], scalar1=cw[:, blk, kind, 1:2]), reads=[rawb[j], bw], writes=[outb])
            S.op("dve", lambda e, j=j: e.scalar_tensor_tensor(out=out, in0=raw[j][:, 0:n], scalar=cw[:, blk, kind, 0:1], in1=out, op0=ALU.mult, op1=ALU.add), reads=[rawb[j], bw], writes=[outb])
            S.op("dve", lambda e, j=j: e.scalar_tensor_tensor(out=out, in0=raw[j][:, 2:n + 2], scalar=cw[:, blk, kind, 2:3], in1=out, op0=ALU.mult, op1=ALU.add), reads=[rawb[j], bw], writes=[outb])

        seqs = [(0, L, 0, 1)]
        if need_ctx:
            seqs.append((L, LC, 2, 3))
        zb = [S.sb([128, 2048], BF16) for _ in range(2)]
        zbb = [Buf() for _ in range(2)]
        fst = [S.sb([128, 512], BF16) for _ in range(2)]
        fstb = [Buf() for _ in range(2)]
        xin = [[S.sb([64, 16, 128], BF16) for _ in range(3)] for _ in range(2)]
        xinb = [[Buf() for _ in range(3)] for _ in range(2)]
        ysb = [S.sb([64, 16, 128], F32) for _ in range(2)]
        ysbb = [Buf() for _ in range(2)]
        tq = [[S.sb([128, 4, 128], F32) for _ in range(4)] for _ in range(2)]
        tqb = [[Buf() for _ in range(4)] for _ in range(2)]
        Bq = [[[S.sb([128, 4, 128], BF16) for _ in range(2)] for _ in range(2)] for _ in range(2)]
        Bqb = [[[Buf() for _ in range(2)] for _ in range(2)] for _ in range(2)]
        Hs = [[S.sb([128, 4, 128], F32) for _ in range(2)] for _ in range(2)]
        bH = [Buf() for _ in range(2)]
        bzsc, bfsc, bysc = Buf(), Buf(), Buf()
        fc = {"zb": 0, "sb": 0}
        TWr = self.TWre.unsqueeze(1).broadcast_to([128, 4, 128])
        TWi = self.TWim.unsqueeze(1).broadcast_to([128, 4, 128])

        def cmul(sl, are, aim, bre, bim, rd, conj, outre, outim, outb_re, outb_im):
            t1, t2, t3, t4 = tq[sl]
            b1, b2, b3, b4 = tqb[sl]
            S.op("dve", lambda e: e.tensor_tensor(out=t1[:], in0=are, in1=bre, op=ALU.mult), reads=rd, writes=[b1])
            S.op("dve", lambda e: e.tensor_tensor(out=t2[:], in0=aim, in1=bim, op=ALU.mult), reads=rd, writes=[b2])
            yield
            if not conj:
                S.op("pool", lambda e: e.tensor_tensor(out=outre, in0=t1[:], in1=t2[:], op=ALU.subtract), reads=[b1, b2], writes=[outb_re])
            else:
                S.op("pool", lambda e: e.tensor_tensor(out=outre, in0=t1[:], in1=t2[:], op=ALU.add), reads=[b1, b2], writes=[outb_re])
            S.op("dve", lambda e: e.tensor_tensor(out=t3[:], in0=are, in1=bim, op=ALU.mult), reads=rd, writes=[b3])
            S.op("dve", lambda e: e.tensor_tensor(out=t4[:], in0=aim, in1=bre, op=ALU.mult), reads=rd, writes=[b4])
            yield
            if not conj:
                S.op("pool", lambda e: e.tensor_tensor(out=outim, in0=t3[:], in1=t4[:], op=ALU.add), reads=[b3, b4], writes=[outb_im])
            else:
                S.op("pool", lambda e: e.tensor_tensor(out=outim, in0=t4[:], in1=t3[:], op=ALU.subtract), reads=[b3, b4], writes=[outb_im])
            yield

        def fft_job(q, g):
            def gen(sl):
                PAC = self.P2[0] if sl == 0 else self.P2[2]
                pb = [self.psb[0], self.psb[1]] if sl == 0 else [self.psb[4], self.psb[5]]
                xr, xi = (2, 3) if sl == 0 else (6, 7)
                Xre = self.ps[xr].rearrange("p (c k) -> p c k", c=4)
                Xim = self.ps[xi].rearrange("p (c k) -> p c k", c=4)
                Hre, Him = Hs[sl]
                nb = [0]

                def fwd(src, srcb):
                    for c4 in range(4):
                        S.op("pe", lambda e, c4=c4: e.matmul(PAC[:, c4 * 256:(c4 + 1) * 256], lhsT=src[:, g * 4 + c4, :], rhs=self.F1cat, start=True, stop=True),
                             reads=[srcb, self.bconst], writes=pb)
                    yield
                    Av = PAC[:, :].rearrange("p (c r k) -> p c r k", c=4, r=2)
                    k = nb[0] % 2
                    nb[0] += 1
                    bre, bim = Bq[sl][k]
                    bbre, bbim = Bqb[sl][k]
                    yield from cmul(sl, Av[:, :, 0, :], Av[:, :, 1, :], TWr, TWi, pb + [self.bconst], False, bre[:], bim[:], bbre, bbim)
                    br2 = bre[:].rearrange("p c k -> p (c k)")
                    bi2 = bim[:].rearrange("p c k -> p (c k)")
                    S.op("pe", lambda e: e.matmul(self.ps[xr], lhsT=self.F2re, rhs=br2, start=True, stop=False), reads=[bbre, self.bconst], writes=[self.psb[xr]])
                    S.op("pe", lambda e: e.matmul(self.ps[xr], lhsT=self.F2imn, rhs=bi2, start=False, stop=True), reads=[bbim, self.bconst], writes=[self.psb[xr]])
                    S.op("pe", lambda e: e.matmul(self.ps[xi], lhsT=self.F2im, rhs=br2, start=True, stop=False), reads=[bbre, self.bconst], writes=[self.psb[xi]])
                    S.op("pe", lambda e: e.matmul(self.ps[xi], lhsT=self.F2re, rhs=bi2, start=False, stop=True), reads=[bbim, self.bconst], writes=[self.psb[xi]])
                    yield

                yield from fwd(xin[q][1], xinb[q][1])
                S.op("act", lambda e: e.activation(out=Hre[:], in_=Xre, func=AF.Copy), reads=[self.psb[xr]], writes=[bH[sl]])
                S.op("act", lambda e: e.activation(out=Him[:], in_=Xim, func=AF.Copy), reads=[self.psb[xi]], writes=[bH[sl]])
                yield
                yield from fwd(xin[q][2], xinb[q][2])
                S.op("dve", lambda e: e.tensor_tensor(out=Hre[:], in0=Hre[:], in1=Xre, op=ALU.add), reads=[self.psb[xr], bH[sl]], writes=[bH[sl]])
                S.op("dve", lambda e: e.tensor_tensor(out=Him[:], in0=Him[:], in1=Xim, op=ALU.subtract), reads=[self.psb[xi], bH[sl]], writes=[bH[sl]])
                yield
                yield from fwd(xin[q][0], xinb[q][0])
                k = nb[0] % 2
                nb[0] += 1
                yre, yim = Bq[sl][k]
                ybre, ybim = Bqb[sl][k]
                yield from cmul(sl, Xre, Xim, Hre[:], Him[:], [self.psb[xr], self.psb[xi], bH[sl]], False, yre[:], yim[:], ybre, ybim)
                for c4 in range(4):
                    S.op("pe", lambda e, c4=c4: e.matmul(PAC[:, c4 * 256:(c4 + 1) * 256], lhsT=yre[:, c4, :], rhs=self.R1, start=True, stop=False),
                         reads=[ybre, self.bconst], writes=pb)
                    S.op("pe", lambda e, c4=c4: e.matmul(PAC[:, c4 * 256:(c4 + 1) * 256], lhsT=yim[:, c4, :], rhs=self.R2, start=False, stop=True),
                         reads=[ybim, self.bconst], writes=pb)
                yield
                Cv = PAC[:, :].rearrange("p (c r k) -> p c r k", c=4, r=2)
                k2 = nb[0] % 2
                nb[0] += 1
                dre, dim = Bq[sl][k2]
                dbre, dbim = Bqb[sl][k2]
                yield from cmul(sl, Cv[:, :, 0, :], Cv[:, :, 1, :], TWr, TWi, pb + [self.bconst], True, dre[:], dim[:], dbre, dbim)
                yps = self.ps[xr]
                S.op("pe", lambda e: e.matmul(yps[0:64, :], lhsT=self.G1re, rhs=dre[:].rearrange("p c k -> p (c k)"), start=True, stop=False),
                     reads=[dbre, self.bconst], writes=[self.psb[xr]])
                S.op("pe", lambda e: e.matmul(yps[0:64, :], lhsT=self.G1im, rhs=dim[:].rearrange("p c k -> p (c k)"), start=False, stop=True),
                     reads=[dbim, self.bconst], writes=[self.psb[xr]])
                yield
                S.op("act", lambda e: e.activation(out=ysb[q][:, g * 4:(g + 1) * 4, :], in_=yps[0:64, :].rearrange("p (c k) -> p c k", c=4), func=AF.Copy),
                     reads=[self.psb[xr]], writes=[ysbb[q]])
                yield
            return gen

        def fft_conv(blk, o, vf):
            for c0 in range(0, L, 2048):
                q = fc["zb"] % 2
                fc["zb"] += 1
                S.op("act", lambda e, q=q, c0=c0: e.activation(out=zb[q][:], in_=z[:, c0:c0 + 2048], func=AF.Copy), reads=[bz], writes=[zbb[q]])
                S.dma(t.zsc[:, c0:c0 + 2048], zb[q][:], reads=[zbb[q]], writes=[bzsc])
            for dirn in range(2):
                for c0 in range(0, L, 512):
                    j = cnt["h"] % 2
                    cnt["h"] += 1
                    S.dma(hch[j][:, :], t.hd2[vf, :, c0:c0 + 512], writes=[hchb[j]])
                    S.dma(tl[j][:, :], t.tlin[vf, c0:c0 + 512].partition_broadcast(128), writes=[tlb[j]])
                    wc = (o * 2 + dirn) * 256 + blk * 128
                    S.op("pe", lambda e, j=j, wc=wc: e.matmul(self.ps[6 + j], lhsT=w3[:, wc:wc + 128], rhs=hch[j][:, :], start=True, stop=True),
                         reads=[bw, hchb[j]], writes=[self.psb[6 + j]])
                    S.op("act", lambda e, j=j: e.activation(out=win[:, :], in_=tl[j][:, :], func=AF.Exp, scale=nd[:, blk, o:o + 1]), reads=[tlb[j], bw], writes=[bwin])
                    S.op("dve", lambda e, j=j: e.tensor_tensor(out=fst[j][:, :], in0=self.ps[6 + j], in1=win[:, :], op=ALU.mult), reads=[self.psb[6 + j], bwin], writes=[fstb[j]])
                    if dirn == 1 and c0 == 0:
                        S.op("dve", lambda e, j=j: e.memset(fst[j][:, 0:1], 0.0), reads=[fstb[j]], writes=[fstb[j]])
                    S.dma(t.fsc[dirn, :, c0:c0 + 512], fst[j][:, :], reads=[fstb[j]], writes=[bfsc])
            for sbk in range(8):
                q = fc["sb"] % 2
                fc["sb"] += 1
                rows = slice(sbk * 16, sbk * 16 + 16)
                S.dma(xin[q][0][:], t.zsc[rows, :].rearrange("c (a b) -> a c b", b=128), reads=[bzsc], writes=[xinb[q][0]])
                S.dma(xin[q][1][:], t.fsc[0, rows, :].rearrange("c (a b) -> a c b", b=128), reads=[bfsc], writes=[xinb[q][1]])
                S.dma(xin[q][2][:], t.fsc[1, rows, :].rearrange("c (a b) -> a c b", b=128), reads=[bfsc], writes=[xinb[q][2]])
                run_rr([fft_job(q, g) for g in range(4)], 2)
                S.dma(t.ysc[rows, :].rearrange("c (a b) -> a c b", b=128), ysb[q][:], reads=[ysbb[q]], writes=[bysc])
            S.dma(acc[:, :], t.ysc[:, :], reads=[bysc], writes=[bacc])

        def do_seq(blk, col0, Ls, vf, vr):
            chunks = [(c, min(512, Ls - c)) for c in range(0, Ls, 512)]
            for (c0, n) in chunks:
                conv_chunk(blk, 0, E_V, col0, Ls, c0, n, z[:, c0:c0 + n], bz)
            def direct_conv(o):
                for (dirn, vi, off) in ((1, vr, 0), (0, vf, Ls - 1)):
                    for (c0, n) in chunks:
                        j = cnt["h"] % 2
                        cnt["h"] += 1
                        S.dma(hch[j][:, :n], t.hd2[vi, :, c0:c0 + n], writes=[hchb[j]])
                        S.dma(tl[j][:, :n], t.tlin[vi, c0:c0 + n].partition_broadcast(128), writes=[tlb[j]])
                        wc = (o * 2 + dirn) * 256 + blk * 128
                        S.op("pe", lambda e, j=j, n=n, wc=wc: e.matmul(self.ps[2 + j][:, :n], lhsT=w3[:, wc:wc + 128], rhs=hch[j][:, :n], start=True, stop=True),
                             reads=[bw, hchb[j]], writes=[self.psb[2 + j]])
                        S.op("act", lambda e, j=j, n=n, o=o: e.activation(out=win[:, :n], in_=tl[j][:, :n], func=AF.Exp, scale=nd[:, blk, o:o + 1]), reads=[tlb[j], bw], writes=[bwin])
                        S.op("dve", lambda e, j=j, n=n, off=off, c0=c0: e.tensor_tensor(out=hk2[:, off + c0:off + c0 + n], in0=self.ps[2 + j][:, :n], in1=win[:, :n], op=ALU.mult),
                             reads=[self.psb[2 + j], bwin], writes=[bhk])
                for s_ in range(Ls):
                    if s_ == 0:
                        S.op("dve", lambda e: e.tensor_scalar_mul(out=acc[:, :Ls], in0=hk2[:, Ls - 1:2 * Ls - 1], scalar1=z[:, 0:1]), reads=[bz, bhk], writes=[bacc])
                    else:
                        S.op("dve", lambda e, s_=s_: e.scalar_tensor_tensor(out=acc[:, :Ls], in0=hk2[:, Ls - 1 - s_:2 * Ls - 1 - s_], scalar=z[:, s_:s_ + 1], in1=acc[:, :Ls], op0=ALU.mult, op1=ALU.add),
                             reads=[bz, bhk], writes=[bacc])
            def do_order(o):
                if Ls == L and self.use_fft:
                    fft_conv(blk, o, vf)
                else:
                    direct_conv(o)
                for (c0, n) in chunks:
                    conv_chunk(blk, o + 1, E_X1 if o == 0 else E_X2, col0, Ls, c0, n, xc[:, :n], bxc)
                    S.op("dve", lambda e, c0=c0, n=n, o=o: e.scalar_tensor_tensor(out=tmp[:, :n], in0=z[:, c0:c0 + n], scalar=dd[:, blk, 1, o:o + 1], in1=acc[:, c0:c0 + n], op0=ALU.mult, op1=ALU.add),
                         reads=[bz, bacc, bw], writes=[btmp])
                    S.op("dve", lambda e, c0=c0, n=n: e.tensor_tensor(out=z[:, c0:c0 + n], in0=xc[:, :n], in1=tmp[:, :n], op=ALU.mult), reads=[bxc, btmp], writes=[bz])
            do_order(0)
            do_order(1)
            for (c0, n) in chunks:
                j = cnt["raw"] % 2
                cnt["raw"] += 1
                q = cnt["ob"] % 2
                cnt["ob"] += 1
                S.dma(raw[j][:, :n], t.uT[E_HG + blk * 128:E_HG + blk * 128 + 128, col0 + c0:col0 + c0 + n], writes=[rawb[j]])
                S.op("act", lambda e, j=j, n=n: e.activation(out=raw[j][:, :n], in_=raw[j][:, :n], func=AF.Silu), reads=[rawb[j]], writes=[rawb[j]])
                S.op("dve", lambda e, j=j, n=n, q=q, c0=c0: e.tensor_tensor(out=ob[q][:, :n], in0=z[:, c0:c0 + n], in1=raw[j][:, :n], op=ALU.mult), reads=[bz, rawb[j]], writes=[obb[q]])
                S.dma(t.gTl[blk * 128:(blk + 1) * 128, col0 + c0:col0 + c0 + n], ob[q][:, :n], reads=[obb[q]])

        for blk in range(2):
            for (col0, Ls, vf, vr) in seqs:
                do_seq(blk, col0, Ls, vf, vr)
        S.barrier()
        S.release(m)

    def qk_prep(self, row0, gain, dst, dstb, rope, with_ctx=True):
        S, t = self.S, self.t
        m = S.mark()
        W_ = 2
        raw = [S.sb([128, 512], F32) for _ in range(W_)]
        rawb = [Buf() for _ in range(W_)]
        sq = [S.sb([128, 512], F32) for _ in range(W_)]
        bsq = [Buf() for _ in range(W_)]
        r = [S.sb([128, 512], F32) for _ in range(W_)]
        br = [Buf() for _ in range(W_)]
        qn = [S.sb([128, 512], F32) for _ in range(W_)]
        bqn = [Buf() for _ in range(W_)]
        cs = [S.sb([128, 2, 512], F32) for _ in range(W_)]
        csb = [Buf() for _ in range(W_)]
        ta = [S.sb([128, 512], F32) for _ in range(W_)]
        bta = [Buf() for _ in range(W_)]

        def job(c0, n):
            def gen(j):
                p1, p2 = 6 + j, 4 + j
                S.dma(raw[j][:, :n], t.uT[row0:row0 + 128, c0:c0 + n], writes=[rawb[j]])
                if rope and c0 < L:
                    S.dma(cs[j][:, 0, :n], t.ropeC[:, c0:c0 + n], writes=[csb[j]])
                    S.dma(cs[j][:, 1, :n], t.ropeS[:, c0:c0 + n], writes=[csb[j]])
                yield
                S.op("act", lambda e: e.activation(out=sq[j][:, :n], in_=raw[j][:, :n], func=AF.Square), reads=[rawb[j]], writes=[bsq[j]])
                yield
                S.op("pe", lambda e: e.matmul(self.ps[p1][:, :n], lhsT=self.ones_f[:], rhs=sq[j][:, :n], start=True, stop=True), reads=[bsq[j], self.bconst], writes=[self.psb[p1]])
                yield
                S.op("dve", lambda e: e.tensor_scalar(out=r[j][:, :n], in0=self.ps[p1][:, :n], scalar1=1.0 / 128, scalar2=EPS, op0=ALU.mult, op1=ALU.add), reads=[self.psb[p1]], writes=[br[j]])
                yield
                S.op("act", lambda e: e.activation(out=r[j][:, :n], in_=r[j][:, :n], func=AF.Sqrt), reads=[br[j]], writes=[br[j]])
                yield
                S.op("dve", lambda e: e.reciprocal(out=r[j][:, :n], in_=r[j][:, :n]), reads=[br[j]], writes=[br[j]])
                yield
                S.op("dve", lambda e: e.scalar_tensor_tensor(out=qn[j][:, :n], in0=raw[j][:, :n], scalar=gain, in1=r[j][:, :n], op0=ALU.mult, op1=ALU.mult),
                     reads=[rawb[j], br[j], self.bconst], writes=[bqn[j]])
                yield
                if rope and c0 < L:
                    S.op("pe", lambda e: e.matmul(self.ps[p2][:, :n], lhsT=self.perm[:], rhs=qn[j][:, :n], start=True, stop=True), reads=[bqn[j], self.bconst], writes=[self.psb[p2]])
                    S.op("dve", lambda e: e.tensor_tensor(out=ta[j][:, :n], in0=qn[j][:, :n], in1=cs[j][:, 0, :n], op=ALU.mult), reads=[bqn[j], csb[j]], writes=[bta[j]])
                    yield
                    S.op("dve", lambda e: e.tensor_tensor(out=qn[j][:, :n], in0=self.ps[p2][:, :n], in1=cs[j][:, 1, :n], op=ALU.mult), reads=[self.psb[p2], csb[j]], writes=[bqn[j]])
                    yield
                    S.op("pool", lambda e: e.tensor_tensor(out=dst[:, c0:c0 + n], in0=ta[j][:, :n], in1=qn[j][:, :n], op=ALU.add), reads=[bta[j], bqn[j]], writes=[dstb])
                else:
                    S.op("pool", lambda e: e.tensor_copy(out=dst[:, c0:c0 + n], in_=qn[j][:, :n]), reads=[bqn[j]], writes=[dstb])
                yield
            return gen

        run_rr([job(c0, n) for (c0, n) in token_chunks()], W_)
        S.barrier()
        S.release(m)

    def attn_alloc(self):
        S = self.S
        A = T()
        A.pt = [S.sb([128, 512], BF16) for _ in range(4)]
        A.ptb = [Buf() for _ in range(4)]
        A.acc = [S.sb([128, 512], F32) for _ in range(2)]
        A.accb = [Buf() for _ in range(2)]
        A.tb = [S.sb([128, 512], F32) for _ in range(4)]
        A.tbb = [Buf() for _ in range(4)]
        A.sb_ = [S.sb([128, 512], F32) for _ in range(2)]
        A.sbb = [Buf() for _ in range(2)]
        A.rinv = S.sb([128, 512], F32)
        A.rinvb = Buf()
        A.o = S.sb([128, 512], F32)
        A.ob_ = Buf()
        A.g = [S.sb([128, 512], F32) for _ in range(2)]
        A.gb = [Buf() for _ in range(2)]
        A.outb = [S.sb([128, 512], BF16) for _ in range(2)]
        A.outbb = [Buf() for _ in range(2)]
        A.k = 0
        A.call = 0
        return A

    def attention(self, A, QT, QTb, KT, KTb, Vt, Vtb, c0, n, keytiles, gate_row0, out_row0, split_acc=True):
        S, t = self.S, self.t
        call = A.call
        A.call += 1
        po = 2 + call % 2
        pr = 4 + call % 2
        nk = len(keytiles)
        k0 = A.k
        A.k += nk
        STP = [0, 1, 6, 7]
        LA = 2
        nacc = [0, 0]

        def issue_st(idx):
            kt, bt = keytiles[idx]
            kk = k0 + idx
            pj = STP[kk % 4]
            S.op("pe", lambda e: e.matmul(self.ps[pj][:, :n], lhsT=KT[:, kt * 128:(kt + 1) * 128], rhs=QT[:, c0:c0 + n], start=True, stop=True),
                 reads=[KTb, QTb], writes=[self.psb[pj]])
            if bt is not None:
                S.dma(A.tb[kk % 4][:, :n], bt, writes=[A.tbb[kk % 4]])

        def issue_rest(idx):
            kt, bt = keytiles[idx]
            kk = k0 + idx
            pj = STP[kk % 4]
            p4 = kk % 4
            if bt is not None:
                p2 = kk % 2
                S.op("dve", lambda e: e.scalar_tensor_tensor(out=A.sb_[p2][:, :n], in0=self.ps[pj][:, :n], scalar=SCALE, in1=A.tb[p4][:, :n], op0=ALU.mult, op1=ALU.add),
                     reads=[self.psb[pj], A.tbb[p4]], writes=[A.sbb[p2]])
                S.op("act", lambda e: e.activation(out=A.pt[p4][:, :n], in_=A.sb_[p2][:, :n], func=AF.Exp), reads=[A.sbb[p2]], writes=[A.ptb[p4]])
            else:
                S.op("act", lambda e: e.activation(out=A.pt[p4][:, :n], in_=self.ps[pj][:, :n], func=AF.Exp, scale=SCALE), reads=[self.psb[pj]], writes=[A.ptb[p4]])
            ai = (idx % 2) if split_acc else 0
            eng = "pool" if ai == 0 else "dve"
            if nacc[ai] == 0:
                S.op(eng, lambda e: e.tensor_copy(out=A.acc[ai][:, :n], in_=A.pt[p4][:, :n]), reads=[A.ptb[p4]], writes=[A.accb[ai]])
            else:
                S.op(eng, lambda e: e.tensor_tensor(out=A.acc[ai][:, :n], in0=A.acc[ai][:, :n], in1=A.pt[p4][:, :n], op=ALU.add), reads=[A.ptb[p4]], writes=[A.accb[ai]])
            nacc[ai] += 1
            S.op("pe", lambda e: e.matmul(self.ps[po][:, :n], lhsT=Vt[:, kt, :], rhs=A.pt[p4][:, :n], start=(idx == 0), stop=(idx == nk - 1)),
                 reads=[Vtb, A.ptb[p4]], writes=[self.psb[po]])

        for idx in range(min(LA, nk)):
            issue_st(idx)
        for idx in range(nk):
            if idx + LA < nk:
                issue_st(idx + LA)
            issue_rest(idx)
        used = [ai for ai in range(2) if nacc[ai]]
        for j_, ai in enumerate(used):
            S.op("pe", lambda e, ai=ai, j_=j_: e.matmul(self.ps[pr][:, :n], lhsT=self.ones_f[:], rhs=A.acc[ai][:, :n], start=(j_ == 0), stop=(j_ == len(used) - 1)),
                 reads=[A.accb[ai], self.bconst], writes=[self.psb[pr]])
        S.op("dve", lambda e: e.reciprocal(out=A.rinv[:, :n], in_=self.ps[pr][:, :n]), reads=[self.psb[pr]], writes=[A.rinvb])
        S.op("dve", lambda e: e.tensor_tensor(out=A.o[:, :n], in0=self.ps[po][:, :n], in1=A.rinv[:, :n], op=ALU.mult), reads=[self.psb[po], A.rinvb], writes=[A.ob_])
        j = call % 2
        S.dma(A.g[j][:, :n], t.uT[gate_row0:gate_row0 + 128, c0:c0 + n], writes=[A.gb[j]])
        S.op("act", lambda e: e.activation(out=A.g[j][:, :n], in_=A.g[j][:, :n], func=AF.Silu), reads=[A.gb[j]], writes=[A.gb[j]])
        S.op("dve", lambda e: e.tensor_tensor(out=A.outb[j][:, :n], in0=A.o[:, :n], in1=A.g[j][:, :n], op=ALU.mult), reads=[A.ob_, A.gb[j]], writes=[A.outbb[j]])
        S.dma(t.gTl[out_row0:out_row0 + 128, c0:c0 + n], A.outb[j][:, :n], reads=[A.outbb[j]])

    def stage_gqa(self, layer, need_ctx):
        S, t = self.S, self.t
        m = S.mark()
        KT = S.sb([128, NT], BF16, "KT")
        QT = S.sb([128, NT], BF16, "QT")
        Vt = S.sb([128, 66, 128], BF16, "Vt")
        KTb, QTb, Vtb = Buf(), Buf(), Buf()
        for q6 in range(6):
            S.dma(Vt[:, q6 * 11:(q6 + 1) * 11, :], t.uV[:, 0:128].rearrange("(kt p) d -> p kt d", p=128)[:, q6 * 11:(q6 + 1) * 11, :], writes=[Vtb])
        self.qk_prep(E_K, self.qkn[:, layer, 1:2], KT, KTb, True)
        A = self.attn_alloc()
        allk = [(kt, None) for kt in range(66)]
        for hq in range(2):
            self.qk_prep(E_Q + hq * 128, self.qkn[:, layer, 0:1], QT, QTb, True)
            for qc in range(16):
                self.attention(A, QT, QTb, KT, KTb, Vt, Vtb, qc * 512, 512, allk, E_AG + hq * 128, 256 + hq * 128)
            if need_ctx:
                self.attention(A, QT, QTb, KT, KTb, Vt, Vtb, L, LC, [(64, None), (65, None)], E_AG + hq * 128, 256 + hq * 128)
        S.barrier()
        S.release(m)

    def stage_na(self, layer, need_ctx):
        S, t = self.S, self.t
        i = layer // 2
        m = S.mark()
        KT = S.sb([128, NT], BF16, "KT")
        QT = S.sb([128, NT], BF16, "QT")
        Vt = S.sb([128, 66, 128], BF16, "Vt")
        KTb, QTb, Vtb = Buf(), Buf(), Buf()
        A = self.attn_alloc()
        for h in range(4):
            for q6 in range(6):
                S.dma(Vt[:, q6 * 11:(q6 + 1) * 11, :], t.uV[:, h * 128:(h + 1) * 128].rearrange("(kt p) d -> p kt d", p=128)[:, q6 * 11:(q6 + 1) * 11, :], writes=[Vtb])
            self.qk_prep(O_K + h * 128, self.qkn[:, layer, 1:2], KT, KTb, False)
            self.qk_prep(O_Q + h * 128, self.qkn[:, layer, 0:1], QT, QTb, False)
            for qb in range(16):
                kb = min(max(8 * qb - 4, 0), 112)
                var = 0 if qb == 0 else (2 if qb == 15 else 1)
                kts = [(kb // 2 + mm, t.btab[i][h, var, mm, :, :]) for mm in range(8)] + [(64, None), (65, None)]
                self.attention(A, QT, QTb, KT, KTb, Vt, Vtb, qb * 512, 512, kts, O_G + h * 128, h * 128, split_acc=False)
            if need_ctx:
                self.attention(A, QT, QTb, KT, KTb, Vt, Vtb, L, LC, [(64, None), (65, None)], O_G + h * 128, h * 128)
        S.barrier()
        S.release(m)

    def dump(self, name, src, shape, dt=F32):
        dst = self.nc.dram_tensor(name, list(shape), dt, kind="ExternalOutput").ap()
        self.S.barrier()
        self.S.dma(dst, src)
        self.S.barrier()

    def build(self, dbg=None, stop=None):
        S, t = self.S, self.t

        def after(name, layer):
            if dbg:
                dbg(self, name, layer)
            return stop == (name, layer)

        self.consts()
        self.stage_ada()
        done = after("ada", 0)
        src = t.xT
        for layer in range(self.depth):
            if done:
                break
            need_ctx = layer < 3
            self.stage_norm(layer, src)
            if after("norm", layer):
                break
            self.stage_proj(layer)
            if after("proj", layer):
                break
            if layer % 2 == 0:
                self.stage_hyena(layer, need_ctx)
                if after("hyena", layer):
                    break
                self.stage_gqa(layer, need_ctx)
                if after("gqa", layer):
                    break
            else:
                self.stage_na(layer, need_ctx)
                if after("na", layer):
                    break
            self.allgather(t.gTl[:, :], t.gT[:, :])
            self.stage_out(layer, src, t.xs, last=(layer == self.depth - 1))
            if after("out", layer):
                break
            src = t.xs
        S.barrier()
        S.emit()


def _const_tables():
    f32 = np.float32
    out = {}
    zs, tl = [], []
    for Ls in (L, LC):
        tt = np.linspace(0.0, 1.0, Ls, dtype=f32)[:, None]
        w = (f32(2.0 * np.pi / Ls) * np.arange(Ls, dtype=f32))[:, None]
        bands = np.linspace(1e-4, 15, 16, dtype=f32)[None, :]
        z = np.concatenate([tt, np.cos(bands * w), -np.sin(bands * w)], axis=-1).astype(f32)
        for rev in (False, True):
            zz = z[::-1] if rev else z
            buf = np.zeros((33, L), f32)
            buf[:, :Ls] = zz.T
            zs.append(buf)
            tb = np.zeros((L,), f32)
            tb[:Ls] = tt[::-1, 0] if rev else tt[:, 0]
            tl.append(tb)
    out["zT"] = np.stack(zs)
    out["tlin"] = np.stack(tl)
    tpos = np.arange(L)
    rows = (tpos // 64).astype(f32)
    cols = (tpos % 64).astype(f32)
    inv = (f32(10000.0) ** (-np.arange(32, dtype=f32) / f32(32))).astype(f32)
    C = np.zeros((128, L), f32)
    Sg = np.zeros((128, L), f32)
    perm = np.zeros((128, 128), f32)
    for d in range(128):
        pos = rows if d < 64 else cols
        ang = (pos * inv[d % 32]).astype(f32)
        C[d] = np.cos(ang)
        first = (d % 64) < 32
        Sg[d] = -np.sin(ang) if first else np.sin(ang)
        partner = d + 32 if first else d - 32
        perm[partner, d] = 1.0
    out["ropeC"], out["ropeS"], out["perm"] = C, Sg, perm
    k = np.arange(128, dtype=np.float64)
    ang128 = 2.0 * np.pi * np.outer(k, k) / 128.0
    angN = 2.0 * np.pi * np.outer(k, k) / 16384.0
    dft = np.zeros((128, 1536), np.float64)
    dft[:64, 0:128] = np.cos(ang128[:64]); dft[:64, 128:256] = -np.sin(ang128[:64])
    dft[:, 256:384] = np.cos(angN); dft[:, 384:512] = -np.sin(angN)
    dft[:, 512:640] = np.cos(ang128); dft[:, 640:768] = -np.sin(ang128); dft[:, 768:896] = np.sin(ang128)
    dft[:, 896:1024] = np.cos(ang128); dft[:, 1024:1152] = np.sin(ang128)
    dft[:, 1152:1280] = -np.sin(ang128); dft[:, 1280:1408] = np.cos(ang128)
    dft[:, 1408:1472] = np.cos(ang128[:, :64]) / 16384.0; dft[:, 1472:1536] = -np.sin(ang128[:, :64]) / 16384.0
    out["dft"] = dft.astype(f32)
    return out


def _btab(rel_bias_heads):
    nh = rel_bias_heads.shape[0]
    tab = np.empty((nh, 3, 8, 128, 512), np.float32)
    p = np.arange(128)[:, None]
    f = np.arange(512)[None, :]
    for vi, qb in enumerate((0, 1, 15)):
        r0 = 8 * qb
        kb = min(max(r0 - 4, 0), 112)
        for m in range(8):
            kr = kb + 2 * m + p // 64
            kc = p % 64
            qr = r0 + f // 64
            qc = f % 64
            rs = np.clip(qr - 4, 0, 120)
            cs = np.clip(qc - 8, 0, 48)
            valid = (kr >= rs) & (kr < rs + 8) & (kc >= cs) & (kc < cs + 16)
            di = np.clip(kr - qr + 7, 0, 14)
            dj = np.clip(kc - qc + 15, 0, 30)
            g = rel_bias_heads[:, di, dj]
            tab[:, vi, m] = np.where(valid[None], g, np.float32(-30000.0))
    return tab


def _prep_inputs(inp):
    f32 = np.float32
    A = lambda k: np.asarray(inp[k], dtype=f32)
    x, c, ctx, c_ctx = A("x"), A("c"), A("ctx"), A("c_ctx")
    norm_g, ada_w, ada_b = A("norm_g"), A("ada_w"), A("ada_b")
    e_w_in, e_w_out, o_w_in, o_w_out = A("e_w_in"), A("e_w_out"), A("o_w_in"), A("o_w_out")
    consts = _const_tables()
    cvec = np.stack([c[0], c_ctx])
    cvT = np.ascontiguousarray(cvec.reshape(2, 32, 128).transpose(2, 0, 1))
    xfull = np.concatenate([x[0], ctx[0]], axis=0)
    maps = []
    for r in range(NCORES):
        d = dict(consts)
        fs = slice(FO * r, FO * r + FO)
        d["xT"] = np.ascontiguousarray(xfull[:, fs].T)
        d["cvT"] = cvT
        d["normg"] = np.ascontiguousarray(norm_g[:, fs].reshape(4, 4, 128).transpose(2, 0, 1))
        aw = ada_w.reshape(4, D, 3, D)[:, :, :, fs]
        d["adaw"] = np.ascontiguousarray(aw.reshape(4 * D, 3 * FO))
        d["adab"] = np.ascontiguousarray(ada_b.reshape(4, 3, D)[:, :, fs].reshape(4, 3, 4, 128).transpose(3, 0, 1, 2))
        qkn = np.zeros((128, 4, 2), f32)
        for l in range(4):
            i = l // 2
            if l % 2 == 0:
                ch = slice(256 * r, 256 * r + 256)
                kv = r // 2
                cols = np.concatenate([
                    np.arange(2048)[ch], 2048 + np.arange(2048)[ch], 4096 + np.arange(2048)[ch], 6144 + np.arange(2048)[ch],
                    8192 + 256 * r + np.arange(256), 10240 + 128 * kv + np.arange(128), 11264 + 256 * r + np.arange(256),
                    10752 + 128 * kv + np.arange(128)])
                d[f"win{l}"] = np.ascontiguousarray(e_w_in[i][:, cols])
                rowperm = np.concatenate([np.concatenate([256 * rr + np.arange(256), 2048 + 256 * rr + np.arange(256)]) for rr in range(8)])
                d[f"wout{l}"] = np.ascontiguousarray(e_w_out[i][rowperm][:, fs])
                qkn[:, l, 0] = A("gqa_q_norm")[i]
                qkn[:, l, 1] = A("gqa_k_norm")[i]
            else:
                cols = np.concatenate([512 * r + np.arange(512), 4096 + 512 * r + np.arange(512), 3 * 4096 + 512 * r + np.arange(512),
                                       2 * 4096 + 512 * r + np.arange(512)])
                d[f"win{l}"] = np.ascontiguousarray(o_w_in[i][:, cols])
                d[f"wout{l}"] = np.ascontiguousarray(o_w_out[i][:, fs])
                qkn[:, l, 0] = A("na_q_norm")[i]
                qkn[:, l, 1] = A("na_k_norm")[i]
        d["qkn"] = qkn
        for i in range(2):
            ch = 256 * r
            hc = A("hy_conv")[i].reshape(3, 3, 2048)[:, :, ch:ch + 256]
            d[f"hyconv{i}"] = np.ascontiguousarray(hc.reshape(3, 3, 2, 128).transpose(3, 2, 1, 0))
            d[f"hyw1_{i}"] = A("hy_w1")[i]
            d[f"hyw2_{i}"] = A("hy_w2")[i]
            d[f"hyvec{i}"] = np.ascontiguousarray(np.stack([A("hy_b1")[i], A("hy_b2")[i], A("hy_freq")[i]], axis=1))
            w3 = A("hy_w3")[i].reshape(64, 2, 2, 2048)[:, :, :, ch:ch + 256]
            d[f"hyw3_{i}"] = np.ascontiguousarray(w3.reshape(64, 1024))
            dec = A("hy_decay")[i][:, ch:ch + 256].reshape(2, 2, 128)
            dsk = A("hy_dskip")[i][:, ch:ch + 256].reshape(2, 2, 128)
            d[f"hydd{i}"] = np.ascontiguousarray(np.stack([dec, dsk], axis=0).transpose(3, 2, 0, 1))
            d[f"btab{i}"] = _btab(A("na_rel_bias")[i][4 * r:4 * r + 4])
        maps.append(d)
    return maps


_CACHE = {}


def kernel(**inputs):
    maps = _prep_inputs(inputs)
    if "nc" not in _CACHE:
        nc = bass.Bass("TRN2", target_bir_lowering=False)
        kb = K(nc)
        kb.build()
        _CACHE["nc"] = nc
    nc = _CACHE["nc"]
    res = run_bass_kernel_spmd(nc, maps, core_ids=list(range(NCORES)))
    out = np.empty((1, L, D), np.float32)
    for r in range(NCORES):
        out[0][:, FO * r:FO * r + FO] = np.asarray(res.results[r]["outT"]).T
    return out
```
